# Optimizing a Trainium2 kernel written in Bass

```python
import jax, jax.numpy as jnp
from jax import lax
import numpy as np


D_MODEL = 2048
BATCH = 4
SEQ = 2048
DEPTH = 2
DEC_BATCH = 128
DEC_SEQ = 1
PAST_LEN = 16384
PAGE_SIZE = 128

D_MIX_A = D_MODEL // 2
A_HEADS = 8
A_HEAD_DIM = D_MIX_A // A_HEADS
CHUNK = 128
D_MIX_B = D_MODEL // 2
POOL_WINDOWS = (2, 4, 8, 16)
POOL_GROUPS = 4
POOL_GROUP_DIM = D_MIX_B // POOL_GROUPS
POOL_BUF = 15
D_MIX_C = D_MODEL // 2
CONV_WIDTH = 31
CONV_BUF = CONV_WIDTH - 1
N_BRANCH = 3
N_IN = 2 * D_MIX_A + D_MIX_B + 2 * D_MIX_C + N_BRANCH * D_MODEL
D_FF = ((8 * D_MODEL // 3 + 255) // 256) * 256
LN_EPS = 1e-5

kernel_name = "gated_hybrid_chunkmlp_pool_conformer_step"


def layer_norm(x, g, b):
    xf = x.astype(jnp.float32)
    mu = jnp.mean(xf, axis=-1, keepdims=True)
    xc = xf - mu
    var = jnp.mean(xc * xc, axis=-1, keepdims=True)
    y = xc * lax.rsqrt(var + LN_EPS) * g.astype(jnp.float32) + b.astype(jnp.float32)
    return y.astype(x.dtype)


def chunk_spatial_mix(v, ws, bs):
    b, t, c = v.shape
    n_chunks = -(-t // CHUNK)
    pad = n_chunks * CHUNK - t
    vp = jnp.pad(v, ((0, 0), (0, pad), (0, 0))).reshape(b, n_chunks, CHUNK, A_HEADS, A_HEAD_DIM)
    causal = jnp.tril(jnp.ones((CHUNK, CHUNK), dtype=bool))
    wm = jnp.where(causal[None], ws, 0).astype(v.dtype)
    s = jnp.einsum('hij,bcjhd->bcihd', wm, vp) + jnp.transpose(bs)[None, None, :, :, None]
    return s.reshape(b, n_chunks * CHUNK, c)[:, :t]


def multiscale_pool(xb, prefix, w_group, scale):
    b, t, _ = xb.shape
    p_len = prefix.shape[1]
    ext = jnp.concatenate([prefix, xb], axis=1)
    extf = ext.astype(jnp.float32)
    cs = jnp.pad(jnp.cumsum(extf, axis=1), ((0, 0), (1, 0), (0, 0)))
    pos = jnp.arange(p_len, p_len + t)
    parts = []
    for g, w in enumerate(POOL_WINDOWS):
        lo = jnp.maximum(pos - w + 1, 0)
        cnt = (pos - lo + 1).astype(jnp.float32)
        sl = slice(g * POOL_GROUP_DIM, (g + 1) * POOL_GROUP_DIM)
        win = cs[:, pos + 1, sl] - cs[:, lo, sl]
        parts.append(win / cnt[None, :, None])
    pooled = (jnp.concatenate(parts, axis=-1) - extf[:, p_len:]).astype(xb.dtype)
    pg = pooled.reshape(b, t, POOL_GROUPS, POOL_GROUP_DIM)
    mixed = jnp.einsum('btgc,gcd->btgd', pg, w_group).reshape(b, t, D_MIX_B)
    return mixed * scale, ext[:, -POOL_BUF:]


def causal_depthwise_conv(c, prefix, w_dw, b_dw):
    ext = jnp.concatenate([prefix, c], axis=1)
    out = lax.conv_general_dilated(
        ext, w_dw[:, None, :].astype(ext.dtype), window_strides=(1,), padding='VALID',
        dimension_numbers=('NWC', 'WIO', 'NWC'), feature_group_count=D_MIX_C)
    return out + b_dw, ext[:, -CONV_BUF:]


def decoder_layer(x, pool_prefix, conv_prefix, alpha,
                  w_in, b_in, a_ln_g, a_ln_b, a_ws, a_bs, w_a_out,
                  b_w_group, b_scale, w_b_out,
                  c_w_dw, c_b_dw, c_ln_g, c_ln_b, w_c_out,
                  w_out, ln1_g, ln1_b, w_ffn_up, w_ffn_down, ln2_g, ln2_b):
    h = jnp.einsum('btd,dn->btn', x, w_in) + b_in
    s1 = 2 * D_MIX_A
    s2 = s1 + D_MIX_B
    s3 = s2 + 2 * D_MIX_C
    h_a, h_b, h_c, h_g = h[..., :s1], h[..., s1:s2], h[..., s2:s3], h[..., s3:]
    u, v = jnp.split(jax.nn.gelu(h_a, approximate=False), 2, axis=-1)
    v = layer_norm(v, a_ln_g, a_ln_b)
    y_a = jnp.einsum('btc,cd->btd', u * chunk_spatial_mix(v, a_ws, a_bs), w_a_out)
    pooled, pool_rows = multiscale_pool(h_b, pool_prefix, b_w_group, b_scale)
    y_b = jnp.einsum('btc,cd->btd', pooled, w_b_out)
    c_val, c_gate = jnp.split(h_c, 2, axis=-1)
    conv, conv_rows = causal_depthwise_conv(c_val * jax.nn.sigmoid(c_gate), conv_prefix, c_w_dw, c_b_dw)
    y_c = jnp.einsum('btc,cd->btd', jax.nn.silu(layer_norm(conv, c_ln_g, c_ln_b)), w_c_out)
    gates = jax.nn.sigmoid(h_g).reshape(h_g.shape[:-1] + (N_BRANCH, D_MODEL))
    merged = gates[..., 0, :] * y_a + gates[..., 1, :] * y_b + gates[..., 2, :] * y_c
    mix = jnp.einsum('btc,cd->btd', merged, w_out)
    x = layer_norm(alpha * x + mix, ln1_g, ln1_b)
    f_gate, f_up = jnp.split(jnp.einsum('btd,df->btf', x, w_ffn_up), 2, axis=-1)
    ffn = jnp.einsum('btf,fd->btd', jax.nn.silu(f_gate) * f_up, w_ffn_down)
    x = layer_norm(alpha * x + ffn, ln2_g, ln2_b)
    return x, pool_rows, conv_rows, v


def setup_inputs(seed: int = 0) -> dict:
    key = jax.random.key(seed)
    ks = jax.random.split(key, 26)
    beta = (8.0 * DEPTH) ** -0.25

    def nrm(k, shape, scale):
        return jax.random.normal(k, shape, jnp.float32) * scale

    def gain(k, shape):
        return 1.0 + nrm(k, shape, 0.02)

    L = DEPTH
    return {
        "x_prompt": nrm(ks[0], (BATCH, SEQ, D_MODEL), 1.0),
        "x_sample": nrm(ks[1], (DEC_BATCH, DEC_SEQ, D_MODEL), 1.0),
        "state_pool": nrm(ks[2], (L, DEC_BATCH, POOL_BUF, D_MIX_B), 1.0),
        "state_conv": nrm(ks[3], (L, DEC_BATCH, CONV_BUF, D_MIX_C), 1.0),
        "w_in": nrm(ks[4], (L, D_MODEL, N_IN), D_MODEL ** -0.5),
        "b_in": nrm(ks[5], (L, N_IN), 0.02),
        "a_ln_g": gain(ks[6], (L, D_MIX_A)),
        "a_ln_b": nrm(ks[7], (L, D_MIX_A), 0.02),
        "a_ws": nrm(ks[8], (L, A_HEADS, CHUNK, CHUNK), 0.5 * CHUNK ** -0.5),
        "a_bs": 1.0 + nrm(ks[9], (L, A_HEADS, CHUNK), 0.1),
        "w_a_out": nrm(ks[10], (L, D_MIX_A, D_MODEL), beta * D_MIX_A ** -0.5),
        "b_w_group": nrm(ks[11], (L, POOL_GROUPS, POOL_GROUP_DIM, POOL_GROUP_DIM), POOL_GROUP_DIM ** -0.5),
        "b_scale": 1.0 + nrm(ks[12], (L, D_MIX_B), 0.1),
        "w_b_out": nrm(ks[13], (L, D_MIX_B, D_MODEL), beta * D_MIX_B ** -0.5),
        "c_w_dw": nrm(ks[14], (L, CONV_WIDTH, D_MIX_C), CONV_WIDTH ** -0.5),
        "c_b_dw": nrm(ks[15], (L, D_MIX_C), 0.02),
        "c_ln_g": gain(ks[16], (L, D_MIX_C)),
        "c_ln_b": nrm(ks[17], (L, D_MIX_C), 0.02),
        "w_c_out": nrm(ks[18], (L, D_MIX_C, D_MODEL), beta * D_MIX_C ** -0.5),
        "w_out": nrm(ks[19], (L, D_MODEL, D_MODEL), beta * D_MODEL ** -0.5),
        "ln1_g": gain(ks[20], (L, D_MODEL)),
        "ln1_b": nrm(ks[21], (L, D_MODEL), 0.02),
        "w_ffn_up": nrm(ks[22], (L, D_MODEL, 2 * D_FF), D_MODEL ** -0.5),
        "w_ffn_down": nrm(ks[23], (L, D_FF, D_MODEL), beta * D_FF ** -0.5),
        "ln2_g": gain(ks[24], (L, D_MODEL)),
        "ln2_b": nrm(ks[25], (L, D_MODEL), 0.02),
    }


def reference(x_prompt, x_sample, state_pool, state_conv,
              w_in, b_in, a_ln_g, a_ln_b, a_ws, a_bs, w_a_out,
              b_w_group, b_scale, w_b_out,
              c_w_dw, c_b_dw, c_ln_g, c_ln_b, w_c_out,
              w_out, ln1_g, ln1_b, w_ffn_up, w_ffn_down, ln2_g, ln2_b):
    alpha = (2.0 * DEPTH) ** 0.25
    n_prompt = x_prompt.shape[0]
    yp = x_prompt
    ys = x_sample
    pool_p, conv_p, pool_s, conv_s, v_s = [], [], [], [], []
    for l in range(DEPTH):
        lw = (w_in[l], b_in[l], a_ln_g[l], a_ln_b[l], a_ws[l], a_bs[l], w_a_out[l],
              b_w_group[l], b_scale[l], w_b_out[l],
              c_w_dw[l], c_b_dw[l], c_ln_g[l], c_ln_b[l], w_c_out[l],
              w_out[l], ln1_g[l], ln1_b[l], w_ffn_up[l], w_ffn_down[l], ln2_g[l], ln2_b[l])
        pool0 = jnp.zeros((n_prompt, 0, D_MIX_B), yp.dtype)
        conv0 = jnp.zeros((n_prompt, CONV_BUF, D_MIX_C), yp.dtype)
        yp, pr, cr, _ = decoder_layer(yp, pool0, conv0, alpha, *lw)
        ys, ps, cs, vs = decoder_layer(ys, state_pool[l].astype(ys.dtype), state_conv[l].astype(ys.dtype), alpha, *lw)
        pool_p.append(pr)
        conv_p.append(cr)
        pool_s.append(ps)
        conv_s.append(cs)
        v_s.append(vs)
    new_pool_prompt = jnp.stack(pool_p, axis=0)
    new_conv_prompt = jnp.stack(conv_p, axis=0)
    new_pool_sample = jnp.stack(pool_s, axis=0)
    new_conv_sample = jnp.stack(conv_s, axis=0)
    new_chunk_v_sample = jnp.stack(v_s, axis=0)
    return (yp, ys, new_pool_prompt, new_conv_prompt, new_pool_sample, new_conv_sample, new_chunk_v_sample)
```

```python
import numpy as np
from contextlib import ExitStack
import concourse.bass as bass
import concourse.mybir as mybir
from concourse.bass_utils import run_bass_kernel_spmd

F32 = mybir.dt.float32
BF16 = mybir.dt.bfloat16
AF = mybir.ActivationFunctionType
ALU = mybir.AluOpType
AX = mybir.AxisListType

D = 2048
NIN = 11264
DFF = 5632
DEPTH = 2
ALPHA = (2.0 * DEPTH) ** 0.25
EPS = 1e-5
NCORES = 8
TA = 640
TB = 528
NTOK = TA + TB
POOL_W = (2, 4, 8, 16)
POOL_CONV_CH = (2, 5, 7)

ENGS = ("pe", "act", "dve", "pool", "sp")


class Op:
    __slots__ = ("eng", "fn", "deps", "need_inc", "cnt", "dma", "sem", "val")

    def __init__(self, eng, fn, dma=False):
        self.eng = eng
        self.fn = fn
        self.deps = []
        self.need_inc = False
        self.cnt = None
        self.dma = dma
        self.sem = None
        self.val = None


class Sched:
    def __init__(self, nc, stack):
        self.nc = nc
        self.stack = stack
        self.ops = {e: [] for e in ENGS}
        self.all_ops = []
        self.last_w = {}
        self.readers = {}
        self.eng_sem = {e: stack.enter_context(nc.semaphore("s_" + e)) for e in ENGS}
        self.dma_sems = {}
        self.dma_val = {}
        self.dma_closed = {}

    def _track(self, op, reads, writes):
        deps = set()
        for r in reads:
            w = self.last_w.get(r)
            if w is not None:
                deps.add(w)
        for w_ in writes:
            w = self.last_w.get(w_)
            if w is not None:
                deps.add(w)
            for rd in self.readers.get(w_, {}).values():
                deps.add(rd)
        deps.discard(op)
        op.deps = [("d", d.sem, self.dma_val[d.sem]) if d.dma else d for d in deps]
        for d in deps:
            if d.dma:
                self.dma_closed[d.sem] = self.dma_val[d.sem]
        rkey = ("d", op.sem) if op.dma else op.eng
        for r in reads:
            self.readers.setdefault(r, {})[rkey] = op
        for w_ in writes:
            self.last_w[w_] = op
            self.readers[w_] = {}

    def op(self, eng, fn, reads=(), writes=()):
        o = Op(eng, fn)
        self._track(o, reads, writes)
        self.ops[eng].append(o)
        self.all_ops.append(o)
        return o

    def dma(self, queue, sem, out, in_, reads=(), writes=(), **kw):
        if sem not in self.dma_sems:
            self.dma_sems[sem] = self.stack.enter_context(self.nc.semaphore("d_" + sem))
            self.dma_val[sem] = 0

        def fn(e, out=out, in_=in_, kw=kw):
            return e.dma_start(out=out, in_=in_, **kw)

        o = Op(queue, fn, dma=True)
        o.sem = sem
        self._track(o, reads, writes)
        if self.dma_closed.get(sem):
            o.deps.append(("d", sem, self.dma_closed[sem]))
            self.dma_closed[sem] = None
        self.dma_val[sem] += 16
        o.val = self.dma_val[sem]
        self.ops[queue].append(o)
        self.all_ops.append(o)
        return o

    def emit(self, final_waits):
        nc = self.nc
        for o in self.all_ops:
            for d in o.deps:
                if not isinstance(d, tuple):
                    if d.eng == "pe" and o.eng == "pe":
                        continue
                    d.need_inc = True
        cnt = {e: 0 for e in ENGS}
        for e in ENGS:
            for o in self.ops[e]:
                if not o.dma and o.need_inc:
                    cnt[e] += 1
                    o.cnt = cnt[e]
        sched = self

        def run(ename, e):
            waited = {}
            for o in sched.ops[ename]:
                need = {}
                for d in o.deps:
                    if isinstance(d, tuple):
                        key = ("d", d[1])
                        v = d[2]
                    else:
                        if d.eng == "pe" and ename == "pe":
                            continue
                        key = ("e", d.eng)
                        v = d.cnt
                    if v > need.get(key, 0):
                        need[key] = v
                for key, v in need.items():
                    if waited.get(key, 0) >= v:
                        continue
                    waited[key] = v
                    s = sched.dma_sems[key[1]] if key[0] == "d" else sched.eng_sem[key[1]]
                    e.wait_ge(s, v)
                ins = o.fn(e)
                if o.dma:
                    ins.then_inc(sched.dma_sems[o.sem], 16)
                elif o.need_inc:
                    ins.then_inc(sched.eng_sem[ename], 1)
            if ename == "sp":
                for sem in final_waits:
                    e.wait_ge(sched.dma_sems[sem], sched.dma_val[sem])

        with nc.Block() as block:
            @block.tensor
            def _(e):
                run("pe", e)

            @block.scalar
            def _(e):
                run("act", e)

            @block.vector
            def _(e):
                run("dve", e)

            @block.gpsimd
            def _(e):
                run("pool", e)

            @block.sync
            def _(e):
                run("sp", e)
        return cnt


def MM(out, lhsT, rhs, start, stop):
    return lambda e: e.matmul(out, lhsT=lhsT, rhs=rhs, start=start, stop=stop)


def TR(out, in_, ident):
    return lambda e: e.transpose(out, in_, ident)


def ACT(out, in_, func, bias=None, scale=None):
    kw = {}
    if bias is not None:
        kw["bias"] = bias
    if scale is not None:
        kw["scale"] = scale
    return lambda e: e.activation(out=out, in_=in_, func=func, **kw)


def ACP(out, in_):
    return lambda e: e.copy(out=out, in_=in_)


def TT(out, in0, in1, op):
    return lambda e: e.tensor_tensor(out=out, in0=in0, in1=in1, op=op)


def TS(out, in0, s1, s2, op0, op1=None):
    if op1 is None:
        return lambda e: e.tensor_scalar(out=out, in0=in0, scalar1=s1, scalar2=None, op0=op0)
    return lambda e: e.tensor_scalar(out=out, in0=in0, scalar1=s1, scalar2=s2, op0=op0, op1=op1)


def STT(out, in0, scalar, in1, op0, op1):
    return lambda e: e.scalar_tensor_tensor(out=out, in0=in0, scalar=scalar, in1=in1, op0=op0, op1=op1)


def CP(out, in_):
    return lambda e: e.tensor_copy(out=out, in_=in_)


def MSET(ap, v):
    return lambda e: e.memset(ap, v)


PV_BIN = 0
PV_BSC = 88
PV_CBD = 96
PV_CLG = 104
PV_CLB = 112
PV_WDW = 120
PV_N = 368


def build_program():
    nc = bass.Bass("TRN2", target_bir_lowering=False)

    def din(name, shape):
        return nc.dram_tensor(name, list(shape), F32, kind="ExternalInput").ap()

    def dout(name, shape):
        return nc.dram_tensor(name, list(shape), F32, kind="ExternalOutput").ap()

    xin = din("xin", [NTOK, D])
    maskd = din("mask", [128, 1])
    invcd = din("invc", [128, 64])
    identd = din("ident", [128, 128])
    causd = din("causT", [128, 128])
    st_pool = din("st_pool", [DEPTH, 16, 15, 1024])
    st_conv = din("st_conv", [DEPTH, 16, 30, 1024])
    w_in = din("w_in", [DEPTH, 44, 128, 4096])
    w_a_out = din("w_a_out", [DEPTH, 8, 128, 2048])
    w_b_out = din("w_b_out", [DEPTH, 8, 128, 2048])
    w_c_out = din("w_c_out", [DEPTH, 8, 128, 2048])
    w_out = din("w_out", [DEPTH, 8, 128, 4096])
    w_up = din("w_ffn_up", [DEPTH, 44, 128, 4096])
    w_down = din("w_ffn_down", [DEPTH, 22, 128, 4096])
    b_wg = din("b_w_group", [DEPTH, 128, 2048])
    pvec = din("pvec", [DEPTH, 128, PV_N])
    rowv = din("rowv", [DEPTH, 3, 1024])
    lnv = din("lnv", [DEPTH, 4, D])
    awsT = din("awsT", [DEPTH, 128, 8 * 128])
    absd = din("a_bs", [DEPTH, 8 * 128])
    aw00 = din("a_w00", [DEPTH, 8])
    abs0 = din("a_bs0", [DEPTH, 8])

    yp = dout("yp", [1024, D])
    ys = dout("ys", [16, D])
    o_pool_p = dout("o_pool_p", [DEPTH, 15, 1024])
    o_conv_p = dout("o_conv_p", [DEPTH, 30, 1024])
    o_pool_s = dout("o_pool_s", [DEPTH, 16, 15, 1024])
    o_conv_s = dout("o_conv_s", [DEPTH, 16, 30, 1024])
    o_v_s = dout("o_v_s", [DEPTH, 16, 1024])

    with ExitStack() as st:
        S = Sched(nc, st)

        def sb(name, shape, dt=F32):
            return st.enter_context(nc.sbuf_tensor(name, list(shape), dt))

        x = sb("x", [128, 5, D])
        xT = sb("xT", [128, 16, TA], BF16)
        uT = sb("uT", [128, 8, TA], BF16)
        bmix = sb("bmix", [128, 8, TA], BF16)
        cact = sb("cact", [128, 8, TA], BF16)
        arena = sb("arena", [128, 10480])
        scr = sb("scr", [128, 1310])
        fscr = sb("fscr", [128, 2])
        epsT = sb("epsT", [128, 1])
        lnbc = sb("lnbc", [128, 4096])
        wsl = [sb(f"wsl{i}", [128, 4096], BF16) for i in range(4)]
        pv = sb("pv", [128, DEPTH, PV_N])
        ident = sb("ident_s", [128, 128])
        identb = sb("identb", [128, 128], BF16)
        ones32 = sb("ones32", [128, 128])
        wmT = sb("wmT", [128, 8, 128], BF16)
        wdg = sb("wdg", [128, 8, 128], BF16)
        w00 = sb("w00", [128, 8])
        bs0 = sb("bs0", [128, 8])
        mask = sb("mask_s", [128, 1])
        invc = sb("invc_s", [128, 64])
        pprefix = sb("pprefix", [128, DEPTH, 8, 15])
        cprefix = sb("cprefix", [128, DEPTH, 8, 30])
        stats = [sb(f"stats{i}", [128, 64]) for i in range(5)]
        tmpA = [sb(f"tmpA{i}", [128, 512]) for i in range(2)]
        sq = [sb(f"sq{i}", [128, 320]) for i in range(2)]
        ptA = scr[:, 0:655]
        ptB = scr[:, 655:1310]
        meanb = scr[:, 0:640]
        rstdb = scr[:, 640:1280]
        hT = [scr.bitcast(BF16)[:, i * 1280:(i + 1) * 1280].rearrange("p (a b) -> p a b", b=TA) for i in range(2)]
        gsig = [sb(f"gsig{i}", [128, 320]) for i in range(3)]
        awf = lnbc[:, 0:1024]
        caus = lnbc[:, 1024:1152]
        bsbc = lnbc[:, 3072:4096].rearrange("p (h i) -> p h i", i=128)

        def RG(name, n):
            return [(name, i) for i in range(n)]

        R_XB = RG("xb16", 2)
        R_HT = [("hT", i, c) for i in range(2) for c in range(2)]

        def fence(old, new):
            S.op("dve", MSET(fscr[:, 0:1], 0.0), writes=list(old) + list(new))
        ps = [st.enter_context(nc.psum_tensor(f"ps{i}", [128, 512], F32)) for i in range(8)]
        psb = [p.bitcast(BF16) for p in ps]

        def aview(off, shape, dt=F32):
            n = 1
            for s_ in shape[1:]:
                n *= s_
            if dt == F32:
                ap = arena[:, off:off + n]
            else:
                ap = arena.bitcast(BF16)[:, 2 * off:2 * off + n]
            if len(shape) == 3:
                ap = ap.rearrange("p (a b) -> p a b", b=shape[2])
            return ap

        vraw = aview(0, [128, 5, 1024])
        vbf = aview(5120, [128, 5, 1024], BF16)
        xb16 = [aview(8000 + i * 1024, [128, 2048], BF16) for i in range(2)]
        hbx = aview(0, [128, 8, 15 + TA])
        pooled = aview(5240, [128, 8, TA], BF16)
        glxb = aview(0, [128, 8, 30 + TA], BF16)
        gtail = aview(2680, [128, 8, 46])
        dgb = aview(3048, [128, 31, 128], BF16)
        conv = aview(5360, [128, 8, TA])
        merged = aview(0, [128, 16, TA], BF16)
        macc = aview(5120, [128, 2, TA])
        groups = [
            dict(T=TA, tiles=[(0, 128), (1, 128), (2, 128), (3, 128), (4, 128)], blocks=[(0, 320), (320, 320)],
                 row0=0, sample=False),
            dict(T=TB, tiles=[(0, 128), (1, 128), (2, 128), (3, 128), (4, 16)], blocks=[(0, 264), (264, 264)],
                 row0=TA, sample=True),
        ]

        bank_ctr = [0]

        def bank():
            b = bank_ctr[0] % 8
            bank_ctr[0] += 1
            return b

        slot_ctr = [0]

        def load_panel(src_ap, K, C):
            s = slot_ctr[0] % 4
            slot_ctr[0] += 1
            S.dma("pool", f"w{s}", wsl[s][:, 0:K * C], src_ap, writes=[("w", s)])
            return s, wsl[s][:, 0:K * C].rearrange("p (k c) -> p k c", c=C)

        XT_ALL = [("xT", t) for t in range(5)]

        S.dma("sp", "cst", ident[:], identd, writes=["ident"])
        S.dma("sp", "cst", mask[:], maskd, writes=["mask"])
        S.dma("sp", "cst", invc[:], invcd, writes=["invc"])
        S.dma("sp", "cst", pv[:], pvec.rearrange("l p n -> p l n"), writes=["pv"])
        S.op("dve", CP(identb[:], ident[:]), reads=["ident"], writes=["identb"])
        S.op("dve", MSET(ones32[:], 1.0), writes=["ones32"])
        S.op("dve", MSET(epsT[:], EPS), writes=["epsT"])

        def PV(l, col, n=1):
            return pv[:, l, col:col + n]

        def load_layer_consts(l):
            S.dma("sp", "lnbc", awf, awsT[l], writes=["lnbc"])
            S.dma("sp", "lnbc", caus, causd, writes=[])
            S.dma("sp", "cst", w00[:], aw00[l].partition_broadcast(128), writes=["w00"])
            S.dma("sp", "cst", bs0[:], abs0[l].partition_broadcast(128), writes=["bs0"])
            for h in range(8):
                S.op("dve", TT(wmT[:, h, :], awf[:, h * 128:(h + 1) * 128], caus, ALU.mult),
                     reads=["lnbc"], writes=["wmT"])
                S.op("dve", TS(wdg[:, h, :], ident[:], w00[:, h:h + 1], None, ALU.mult),
                     reads=["ident", "w00"], writes=["wdg"])

        def build_xT(tl):
            for i, (t, R) in enumerate(tl):
                xb = xb16[i % 2]
                S.op("act", ACP(xb[0:R, :], x[0:R, t, :]), reads=[("x", t)], writes=[("xb16", i % 2)])
                for half in range(2):
                    b = bank()
                    for kk in range(8):
                        k = half * 8 + kk
                        S.op("pe", TR(psb[b][:, kk * 128:kk * 128 + R], xb[0:R, k * 128:(k + 1) * 128], identb[0:R, 0:R]),
                             reads=[("xb16", i % 2), "identb"], writes=[("ps", b)])
                    src = psb[b][:, 0:1024].rearrange("p (k c) -> p k c", c=128)[:, :, 0:R]
                    S.op("dve", CP(xT[:, half * 8:half * 8 + 8, t * 128:t * 128 + R], src),
                         reads=[("ps", b)], writes=[("xT", t)])

        def mm_fm(Wv, s, K, cc, src, src_regs, c0, n):
            if src_regs and src_regs[0][0] == "xT":
                src_regs = [("xT", t) for t in range(c0 // 128, (c0 + n - 1) // 128 + 1)]
            b = bank()
            for k in range(K):
                S.op("pe", MM(ps[b][:, 0:n], Wv[:, k, cc * 128:(cc + 1) * 128], src[:, k, c0:c0 + n], k == 0, k == K - 1),
                     reads=[("w", s)] + src_regs, writes=[("ps", b)])
            return b

        def ln_multi(items, Dn, gcol, bcol, pre=0):
            for _ in ln_stages(items, Dn, gcol, bcol, pre):
                pass

        def bn_piece(t, R, c, ap, reg):
            S.op("dve", lambda e: e.bn_stats(out=stats[t][0:R, c * 6:(c + 1) * 6], in_=ap), reads=[reg], writes=[("st", t)])

        def ln_stages(items, Dn, gcol, bcol, pre=0):
            nch = pre if pre else Dn // 512
            its = []
            for (ap_fn, R, reg, ti) in items:
                its.append((ap_fn, R, reg, stats[ti], ("st", ti)))
            for (ap_fn, R, reg, stt, sreg) in its:
                if not pre:
                    for c in range(nch):
                        S.op("dve", lambda e, c=c, stt=stt, R=R, ap_fn=ap_fn: e.bn_stats(out=stt[0:R, c * 6:(c + 1) * 6], in_=ap_fn(c * 512, 512)),
                             reads=[reg], writes=[sreg])
                S.op("dve", lambda e, stt=stt, R=R: e.bn_aggr(out=stt[0:R, 48:50], in_=stt[0:R, 0:nch * 6]), reads=[sreg], writes=[sreg])
            yield 0
            for (ap_fn, R, reg, stt, sreg) in its:
                S.op("act", ACT(stt[0:R, 50:51], stt[0:R, 49:50], AF.Sqrt, bias=epsT[0:R, 0:1], scale=1.0),
                     reads=[sreg, "epsT"], writes=[sreg])
            yield 1
            for (ap_fn, R, reg, stt, sreg) in its:
                S.op("dve", lambda e, stt=stt, R=R: e.reciprocal(out=stt[0:R, 50:51], in_=stt[0:R, 50:51]), reads=[sreg], writes=[sreg])
                S.op("dve", STT(stt[0:R, 51:52], stt[0:R, 48:49], -1.0, stt[0:R, 50:51], ALU.mult, ALU.mult),
                     reads=[sreg], writes=[sreg])
            yield 2
            for (ap_fn, R, reg, stt, sreg) in its:
                full = ap_fn(0, Dn)
                S.op("act", ACT(full, full, AF.Identity, bias=stt[0:R, 51:52], scale=stt[0:R, 50:51]),
                     reads=[reg, sreg], writes=[reg])
            yield 3
            for (ap_fn, R, reg, stt, sreg) in its:
                full = ap_fn(0, Dn)
                S.op("dve", TT(full, full, lnbc[0:R, gcol:gcol + Dn], ALU.mult), reads=[reg, "lnbc"], writes=[reg])
            yield 4
            for (ap_fn, R, reg, stt, sreg) in its:
                full = ap_fn(0, Dn)
                S.op("dve", TT(full, full, lnbc[0:R, bcol:bcol + Dn], ALU.add), reads=[reg, "lnbc"], writes=[reg])
            yield 5

        def layer(gi, l):
            g = groups[gi]
            T = g["T"]
            tiles = g["tiles"]
            blocks = g["blocks"]
            sample = g["sample"]
            TREG = [("xT", t) for t, _ in tiles]
            light = (gi == 0 and l == DEPTH - 1)
            tiles_m = tiles[1:] if light else tiles
            blocks_m = [(128, 256), (384, 256)] if light else blocks
            Tm0 = 128 if light else 0

            def au_panel(p):
                s, Wv = load_panel(w_in[l, p], 16, 256)
                for cc in range(2):
                    ch = p * 2 + cc
                    for (c0, n) in blocks_m:
                        b = mm_fm(Wv, s, 16, cc, xT, TREG, c0, n)
                        S.op("act", ACT(uT[:, ch, c0:c0 + n], ps[b][:, 0:n], AF.Gelu, bias=PV(l, PV_BIN + ch), scale=1.0),
                             reads=[("ps", b), "pv"], writes=[("uT", ch)])

            fence(RG("merged", 16) + RG("macc", 2) + R_XB, RG("vraw", 5) + RG("vbf", 5))
            S.dma("sp", "lnbc", lnbc[:, 0:3072], rowv[l].rearrange("a n -> (a n)").partition_broadcast(128), writes=["lnbc"])
            S.dma("sp", "lnbc", lnbc[:, 3072:4096], absd[l].partition_broadcast(128), writes=[])
            for p in range(4):
                s, Wv = load_panel(w_in[l, 4 + p], 16, 256)
                for (t, R) in tiles_m:
                    b = bank()
                    for k in range(16):
                        S.op("pe", MM(ps[b][0:R, 0:256], xT[:, k, t * 128:t * 128 + R], Wv[:, k, :], k == 0, k == 15),
                             reads=[("w", s), ("xT", t)], writes=[("ps", b)])
                    vs_ = vraw[0:R, t, p * 256:(p + 1) * 256]
                    S.op("dve", TT(vs_, ps[b][0:R, 0:256], lnbc[0:R, 2048 + p * 256:2048 + (p + 1) * 256], ALU.add),
                         reads=[("ps", b), "lnbc"], writes=[("vraw", t)])
                    S.op("act", ACT(vs_, vs_, AF.Gelu), reads=[("vraw", t)], writes=[("vraw", t)])
                    bn_piece(t, R, p, vs_, ("vraw", t))
            au_next = 0
            for stg in ln_stages([(lambda o, n, t=t, R=R: vraw[0:R, t, o:o + n], R, ("vraw", t), t) for (t, R) in tiles_m],
                                 1024, 0, 1024, pre=4):
                if stg in (0, 2, 3, 5):
                    au_panel(au_next)
                    au_next += 1
            for (t, R) in tiles_m:
                S.op("act", ACP(vbf[0:R, t, :], vraw[0:R, t, :]), reads=[("vraw", t)], writes=[("vbf", t)])
            if sample:
                S.dma("sp", "o_vs", o_v_s[l], vraw[0:16, 4, :], reads=[("vraw", 4)])
            for (t, R) in tiles_m:
                is_s = sample and t == 4
                for hh in range(2):
                    b = bank()
                    for h4 in range(4):
                        h = hh * 4 + h4
                        rhs = wdg[0:R, h, 0:R] if is_s else wmT[:, h, :]
                        S.op("pe", MM(ps[b][:, h4 * 128:h4 * 128 + R], vbf[0:R, t, h * 128:(h + 1) * 128], rhs, True, True),
                             reads=[("vbf", t), "wdg" if is_s else "wmT"], writes=[("ps", b)])
                    tm = tmpA[hh]
                    pv3 = ps[b][:, 0:512].rearrange("p (h c) -> p h c", c=128)[:, :, 0:R]
                    tm3 = tm[:, 0:512].rearrange("p (h c) -> p h c", c=128)[:, :, 0:R]
                    bsrc = bsbc[:, hh * 4:hh * 4 + 4, 0:R]
                    if is_s:
                        for h4 in range(4):
                            h = hh * 4 + h4
                            S.op("dve", TS(tm[:, h4 * 128:h4 * 128 + R], ps[b][:, h4 * 128:h4 * 128 + R], bs0[:, h:h + 1], None, ALU.add),
                                 reads=[("ps", b), "bs0"], writes=[("tmpA", hh)])
                    else:
                        S.op("dve", TT(tm3, pv3, bsrc, ALU.add), reads=[("ps", b), "lnbc"], writes=[("tmpA", hh)])
                    dst = uT[:, hh * 4:hh * 4 + 4, t * 128:t * 128 + R]
                    S.op("dve", TT(dst, tm3, dst, ALU.mult), reads=[("tmpA", hh)] + [("uT", hh * 4 + i) for i in range(4)],
                         writes=[("uT", hh * 4 + i) for i in range(4)])

            fence(RG("vraw", 5) + RG("vbf", 5) + R_HT, RG("hbx", 8) + RG("pooled", 8) + ["ptA", "ptB"])
            if gi == 0:
                S.op("dve", MSET(hbx[:, :, 0:15], 0.0), reads=[], writes=[("hbx", c) for c in range(8)])
            else:
                S.op("dve", CP(hbx[:, :, 0:15], pprefix[:, l, :, :]), reads=[("pprefix", l)],
                     writes=[("hbx", c) for c in range(8)])
            if sample:
                S.dma("sp", "lnbc", lnbc[0:120, 0:2048].rearrange("p (a n) -> p a n", a=2),
                      st_pool[l].rearrange("(a b) k n -> (b k) a n", a=2), writes=["lnbc"])
                S.dma("sp", "o_ps", o_pool_s[l][:, 0:14, :], st_pool[l][:, 1:15, :])
            for p in range(4):
                s, Wv = load_panel(w_in[l, 8 + p], 16, 256)
                for cc in range(2):
                    ch = p * 2 + cc
                    for (c0, n) in blocks:
                        b = mm_fm(Wv, s, 16, cc, xT, TREG, c0, n)
                        S.op("act", ACT(hbx[:, ch, 15 + c0:15 + c0 + n], ps[b][:, 0:n], AF.Identity,
                                        bias=PV(l, PV_BIN + 16 + ch), scale=1.0),
                             reads=[("ps", b), "pv"], writes=[("hbx", ch)])
            sg = slot_ctr[0] % 4
            slot_ctr[0] += 1
            WG = wsl[sg][:, 0:2048].rearrange("p (g c d) -> p g c d", g=4, c=2)
            S.dma("pool", f"w{sg}", wsl[sg][:, 0:2048], b_wg[l], writes=[("w", sg)])
            for ch in range(8):
                hreg = ("hbx", ch)
                if gi == 0:
                    S.op("dve", TS(hbx[:, ch, 15:15 + 128], hbx[:, ch, 15:15 + 128], mask[:, 0:1], None, ALU.mult),
                         reads=[hreg, "mask"], writes=[hreg])
                gq = ch // 2
                w = POOL_W[gq]
                ext = hbx[:, ch, :]
                L = 15 + T
                cur = ext
                off = 0
                steps = [(1, ptA), (2, ptB), (4, ptA), (8, ptB)][:gq + 1]
                for (sh, dstbuf) in steps:
                    no = off + sh
                    S.op("dve", TT(dstbuf[:, no:L], cur[:, no:L], cur[:, off:L - sh], ALU.add),
                         reads=[hreg, "ptA", "ptB"], writes=["ptA" if dstbuf is ptA else "ptB"])
                    cur = dstbuf
                    off = no
                S.op("dve", STT(pooled[:, ch, 0:T], cur[:, 15:15 + T], 1.0 / w, ext[:, 15:15 + T], ALU.mult, ALU.subtract),
                     reads=[hreg, "ptA", "ptB"], writes=[("pooled", ch)])
                if gi == 0:
                    S.op("dve", TT(tmpA[0][:, 0:16], cur[:, 15 + 128:15 + 144], invc[:, gq * 16:gq * 16 + 16], ALU.mult),
                         reads=["ptA", "ptB", "invc"], writes=[("tmpA", 0)])
                    S.op("dve", TT(pooled[:, ch, 128:144], tmpA[0][:, 0:16], ext[:, 15 + 128:15 + 144], ALU.subtract),
                         reads=[("tmpA", 0), hreg], writes=[("pooled", ch)])
                    S.op("dve", CP(pprefix[:, l, ch, :], ext[:, T:T + 15]), reads=[hreg], writes=[("pprefix", l)])
                if sample:
                    b = bank()
                    for a in range(2):
                        S.op("pe", TR(ps[b][:, a * 120:(a + 1) * 120], lnbc[0:120, a * 1024 + ch * 128:a * 1024 + (ch + 1) * 128],
                                      ident[0:120, 0:120]), reads=["lnbc", "ident"], writes=[("ps", b)])
                    stv = ps[b][:, 0:240].rearrange("p (b k) -> p b k", k=15)
                    hnew = hbx[:, ch, 15 + 512:15 + 528]
                    if w > 2:
                        S.op("dve", lambda e, stv=stv, w=w: e.reduce_sum(out=tmpA[1][:, 0:16], in_=stv[:, :, 16 - w:15], axis=AX.X),
                             reads=[("ps", b)], writes=[("tmpA", 1)])
                    else:
                        S.op("dve", CP(tmpA[1][:, 0:16], stv[:, :, 14]), reads=[("ps", b)], writes=[("tmpA", 1)])
                    S.op("dve", TT(tmpA[1][:, 0:16], tmpA[1][:, 0:16], hnew, ALU.add), reads=[("tmpA", 1), hreg], writes=[("tmpA", 1)])
                    S.op("dve", STT(pooled[:, ch, 512:528], tmpA[1][:, 0:16], 1.0 / w, hnew, ALU.mult, ALU.subtract),
                         reads=[("tmpA", 1), hreg], writes=[("pooled", ch)])
            if sample:
                b = bank()
                b2 = bank()
                for ch in range(8):
                    bb = b if ch < 4 else b2
                    S.op("pe", TR(ps[bb][0:15, (ch % 4) * 128:(ch % 4 + 1) * 128], hbx[:, ch, 15 + 497:15 + 512], ident[:, :]),
                         reads=[("hbx", ch), "ident"], writes=[("ps", bb)])
                S.op("dve", CP(tmpA[0][0:15, 0:512], ps[b][0:15, 0:512]), reads=[("ps", b)], writes=[("tmpA", 0)])
                S.op("dve", CP(tmpA[1][0:15, 0:512], ps[b2][0:15, 0:512]), reads=[("ps", b2)], writes=[("tmpA", 1)])
                S.dma("sp", "o_pp", o_pool_p[l][:, 0:512], tmpA[0][0:15, 0:512], reads=[("tmpA", 0)])
                S.dma("sp", "o_pp", o_pool_p[l][:, 512:1024], tmpA[1][0:15, 0:512], reads=[("tmpA", 1)])
                b = bank()
                b2 = bank()
                for ch in range(8):
                    bb = b if ch < 4 else b2
                    S.op("pe", TR(ps[bb][0:16, (ch % 4) * 128:(ch % 4 + 1) * 128], hbx[:, ch, 15 + 512:15 + 528], ident[:, :]),
                         reads=[("hbx", ch), "ident"], writes=[("ps", bb)])
                S.op("dve", CP(tmpA[0][0:16, 0:512], ps[b][0:16, 0:512]), reads=[("ps", b)], writes=[("tmpA", 0)])
                S.op("dve", CP(tmpA[1][0:16, 0:512], ps[b2][0:16, 0:512]), reads=[("ps", b2)], writes=[("tmpA", 1)])
                S.dma("sp", "o_ps", o_pool_s[l][:, 14, 0:512], tmpA[0][0:16, 0:512], reads=[("tmpA", 0)])
                S.dma("sp", "o_ps", o_pool_s[l][:, 14, 512:1024], tmpA[1][0:16, 0:512], reads=[("tmpA", 1)])
            for gq in range(4):
                for dd in range(2):
                    ch = gq * 2 + dd
                    for (c0, n) in blocks_m:
                        b = bank()
                        for kc in range(2):
                            S.op("pe", MM(ps[b][:, 0:n], WG[:, gq, kc, dd * 128:(dd + 1) * 128], pooled[:, gq * 2 + kc, c0:c0 + n],
                                          kc == 0, kc == 1),
                                 reads=[("w", sg), ("pooled", gq * 2), ("pooled", gq * 2 + 1)], writes=[("ps", b)])
                        S.op("act", ACT(bmix[:, ch, c0:c0 + n], ps[b][:, 0:n], AF.Identity, scale=PV(l, PV_BSC + ch)),
                             reads=[("ps", b), "pv"], writes=[("bmix", ch)])

            fence(RG("hbx", 8) + RG("pooled", 8) + ["ptA", "ptB"] + R_XB,
                  RG("glxb", 8) + RG("gtail", 8) + ["dg"] + RG("conv", 8) + ["meanb", "rstdb"])
            if gi == 0:
                S.op("dve", MSET(glxb[:, :, 0:30], 0.0), reads=[], writes=RG("glxb", 8))
            else:
                S.op("dve", CP(glxb[:, :, 0:30], cprefix[:, l, :, :]), reads=[("cprefix", l)], writes=RG("glxb", 8))
            if sample:
                S.dma("sp", "lnbc", lnbc[0:120, 0:4096].rearrange("p (a n) -> p a n", a=4),
                      st_conv[l].rearrange("(a b) k n -> (b k) a n", a=4), writes=["lnbc"])
                S.dma("sp", "o_cs", o_conv_s[l][:, 0:29, :], st_conv[l][:, 1:30, :])
            ntail = 46 if sample else 30
            tail_lo = (482 - blocks[1][0]) if sample else (blocks[1][1] - 30)
            gs_ctr = 0
            idb_bc = bass.AP(identb[:, :].tensor, identb[:, :].offset, [list(identb[:, :].ap[0]), [0, 31], [1, 128]])

            def conv_pe(ch):
                wcol = PV_WDW + ch * 31
                for (c0, n) in blocks_m:
                    b = bank()
                    for k in range(31):
                        S.op("pe", MM(ps[b][:, 0:n], dgb[:, k, :], glxb[:, ch, c0 + k:c0 + k + n], k == 0, k == 30),
                             reads=["dg", ("glxb", ch)], writes=[("ps", b)])
                    S.op("act", ACT(conv[:, ch, c0:c0 + n], ps[b][:, 0:n], AF.Identity, bias=PV(l, PV_CBD + ch), scale=1.0),
                         reads=[("ps", b), "pv"], writes=[("conv", ch)])
                if sample:
                    gnew = gtail[:, ch, 30:46]
                    b = bank()
                    for a in range(4):
                        S.op("pe", TR(ps[b][:, a * 120:(a + 1) * 120],
                                      lnbc[0:120, a * 1024 + ch * 128:a * 1024 + (ch + 1) * 128], ident[0:120, 0:120]),
                             reads=["lnbc", "ident"], writes=[("ps", b)])
                    stv = ps[b][:, 0:480].rearrange("p (b k) -> p b k", k=30)
                    tm3 = tmpA[0][:, 0:480].rearrange("p (b k) -> p b k", k=30)
                    S.op("dve", TT(tm3, stv, WB3(pv, l, wcol), ALU.mult), reads=[("ps", b), "pv"], writes=[("tmpA", 0)])
                    S.op("dve", lambda e, tm3=tm3: e.reduce_sum(out=tmpA[1][:, 0:16], in_=tm3, axis=AX.X),
                         reads=[("tmpA", 0)], writes=[("tmpA", 1)])
                    S.op("dve", TS(tmpA[1][:, 16:32], gnew, PV(l, wcol + 30), PV(l, PV_CBD + ch), ALU.mult, ALU.add),
                         reads=[("gtail", ch), "pv"], writes=[("tmpA", 1)])
                    S.op("dve", TT(conv[:, ch, 512:528], tmpA[1][:, 0:16], tmpA[1][:, 16:32], ALU.add),
                         reads=[("tmpA", 1)], writes=[("conv", ch)])

            for ch in range(8):
                p, cc = ch // 2, ch % 2
                if cc == 0:
                    sg_, Wgt_ = load_panel(w_in[l, 16 + p], 16, 256)
                    sv_, Wvl_ = load_panel(w_in[l, 12 + p], 16, 256)
                for bi_, (c0, n) in enumerate(blocks):
                    bg = mm_fm(Wgt_, sg_, 16, cc, xT, TREG, c0, n)
                    gsb = gsig[gs_ctr % 3]
                    greg = ("gsig", gs_ctr % 3)
                    gs_ctr += 1
                    S.op("act", ACT(gsb[:, 0:n], ps[bg][:, 0:n], AF.Sigmoid, bias=PV(l, PV_BIN + 32 + ch), scale=1.0),
                         reads=[("ps", bg), "pv"], writes=[greg])
                    bv = mm_fm(Wvl_, sv_, 16, cc, xT, TREG, c0, n)
                    S.op("dve", STT(glxb[:, ch, 30 + c0:30 + c0 + n], ps[bv][:, 0:n], PV(l, PV_BIN + 24 + ch), gsb[:, 0:n],
                                    ALU.add, ALU.mult),
                         reads=[("ps", bv), greg, "pv"], writes=[("glxb", ch)])
                    if bi_ == 1:
                        S.op("dve", STT(gtail[:, ch, 0:ntail], ps[bv][:, tail_lo:tail_lo + ntail], PV(l, PV_BIN + 24 + ch),
                                        gsb[:, tail_lo:tail_lo + ntail], ALU.add, ALU.mult),
                             reads=[("ps", bv), greg, "pv"], writes=[("gtail", ch)])
                if gi == 0:
                    S.op("dve", TS(glxb[:, ch, 30:30 + 128], glxb[:, ch, 30:30 + 128], mask[:, 0:1], None, ALU.mult),
                         reads=[("glxb", ch), "mask"], writes=[("glxb", ch)])
                    S.op("dve", CP(cprefix[:, l, ch, :], gtail[:, ch, 0:30]), reads=[("gtail", ch)], writes=[("cprefix", l)])
                if ch >= 1:
                    conv_pe(ch - 1)
                wcol = PV_WDW + ch * 31
                wb_ = pv[:, l, wcol:wcol + 31]
                w_bc = bass.AP(wb_.tensor, wb_.offset, [list(wb_.ap[0]), [1, 31], [0, 128]])
                S.op("dve", TT(dgb, idb_bc, w_bc, ALU.mult), reads=["identb", "pv"], writes=["dg"])
            conv_pe(7)
            if sample:
                for (n_r, c_lo, dst_ap, semn) in ((30, 0, o_conv_p[l], "o_cp"), (16, 30, o_conv_s[l][:, 29, :], "o_cs")):
                    b = bank()
                    b2 = bank()
                    for ch in range(8):
                        bb = b if ch < 4 else b2
                        S.op("pe", TR(ps[bb][0:n_r, (ch % 4) * 128:(ch % 4 + 1) * 128], gtail[:, ch, c_lo:c_lo + n_r], ident[:, :]),
                             reads=[("gtail", ch), "ident"], writes=[("ps", bb)])
                    S.op("dve", CP(tmpA[0][0:n_r, 0:512], ps[b][0:n_r, 0:512]), reads=[("ps", b)], writes=[("tmpA", 0)])
                    S.op("dve", CP(tmpA[1][0:n_r, 0:512], ps[b2][0:n_r, 0:512]), reads=[("ps", b2)], writes=[("tmpA", 1)])
                    S.dma("sp", semn, dst_ap[:, 0:512], tmpA[0][0:n_r, 0:512], reads=[("tmpA", 0)])
                    S.dma("sp", semn, dst_ap[:, 512:1024], tmpA[1][0:n_r, 0:512], reads=[("tmpA", 1)])
            CONV_ALL = [("conv", c) for c in range(8)]
            sq_ctr = 0
            for (c0, n) in blocks_m:
                b1 = bank()
                for ch in range(8):
                    S.op("pe", MM(ps[b1][:, 0:n], ones32[:, :], conv[:, ch, c0:c0 + n], ch == 0, ch == 7),
                         reads=[("conv", ch), "ones32"], writes=[("ps", b1)])
                b2 = bank()
                for ch in range(8):
                    sqb = sq[sq_ctr % 2]
                    sreg = ("sq", sq_ctr % 2)
                    sq_ctr += 1
                    S.op("act", ACT(sqb[:, 0:n], conv[:, ch, c0:c0 + n], AF.Square), reads=[("conv", ch)], writes=[sreg])
                    S.op("pe", MM(ps[b2][:, 0:n], ones32[:, :], sqb[:, 0:n], ch == 0, ch == 7),
                         reads=[sreg, "ones32"], writes=[("ps", b2)])
                mb = meanb[:, c0:c0 + n]
                rb = rstdb[:, c0:c0 + n]
                S.op("dve", TS(mb, ps[b1][:, 0:n], 1.0 / 1024, None, ALU.mult), reads=[("ps", b1)], writes=["meanb"])
                S.op("dve", TT(tmpA[0][:, 0:n], mb, mb, ALU.mult), reads=["meanb"], writes=[("tmpA", 0)])
                S.op("dve", STT(rb, ps[b2][:, 0:n], 1.0 / 1024, tmpA[0][:, 0:n], ALU.mult, ALU.subtract),
                     reads=[("ps", b2), ("tmpA", 0)], writes=["rstdb"])
                S.op("dve", TS(rb, rb, 0.0, None, ALU.max), reads=["rstdb"], writes=["rstdb"])
                S.op("act", ACT(rb, rb, AF.Sqrt, bias=epsT[:, 0:1], scale=1.0), reads=["rstdb", "epsT"], writes=["rstdb"])
                S.op("dve", lambda e, rb=rb: e.reciprocal(out=rb, in_=rb), reads=["rstdb"], writes=["rstdb"])
            for ch in range(8):
                cv = conv[:, ch, Tm0:T]
                S.op("dve", TT(cv, cv, meanb[:, Tm0:T], ALU.subtract), reads=[("conv", ch), "meanb"], writes=[("conv", ch)])
                S.op("dve", TT(cv, cv, rstdb[:, Tm0:T], ALU.mult), reads=[("conv", ch), "rstdb"], writes=[("conv", ch)])
                S.op("act", ACT(cact[:, ch, Tm0:T], cv, AF.Silu, bias=PV(l, PV_CLB + ch), scale=PV(l, PV_CLG + ch)),
                     reads=[("conv", ch), "pv"], writes=[("cact", ch)])

            branches = [(w_a_out, uT, [("uT", c) for c in range(8)]),
                        (w_b_out, bmix, [("bmix", c) for c in range(8)]),
                        (w_c_out, cact, [("cact", c) for c in range(8)])]
            fence(RG("glxb", 8) + RG("gtail", 8) + ["dg"] + RG("conv", 8), RG("merged", 16) + RG("macc", 2) + R_XB)
            gs_ctr = 0
            for dp in range(8):
                for bi, (wbr, src, sregs) in enumerate(branches):
                    col = 5120 + bi * 2048 + dp * 256
                    s1, Wg = load_panel(w_in[l, col // 256], 16, 256)
                    s2, Wo = load_panel(wbr[l, dp], 8, 256)
                    for cc in range(2):
                        dc = dp * 2 + cc
                        for (c0, n) in blocks_m:
                            bg = mm_fm(Wg, s1, 16, cc, xT, TREG, c0, n)
                            gsb = gsig[gs_ctr % 3]
                            greg = ("gsig", gs_ctr % 3)
                            gs_ctr += 1
                            S.op("act", ACT(gsb[:, 0:n], ps[bg][:, 0:n], AF.Sigmoid,
                                            bias=PV(l, PV_BIN + 40 + bi * 16 + dc), scale=1.0),
                                 reads=[("ps", bg), "pv"], writes=[greg])
                            by = mm_fm(Wo, s2, 8, cc, src, sregs, c0, n)
                            mreg = ("macc", cc)
                            ma = macc[:, cc, c0:c0 + n]
                            if bi == 0:
                                S.op("dve", TT(ma, ps[by][:, 0:n], gsb[:, 0:n], ALU.mult),
                                     reads=[("ps", by), greg], writes=[mreg])
                            else:
                                S.op("dve", TT(gsb[:, 0:n], ps[by][:, 0:n], gsb[:, 0:n], ALU.mult),
                                     reads=[("ps", by), greg], writes=[greg])
                                if bi == 1:
                                    S.op("dve", TT(ma, ma, gsb[:, 0:n], ALU.add), reads=[greg, mreg], writes=[mreg])
                                else:
                                    S.op("dve", TT(merged[:, dc, c0:c0 + n], ma, gsb[:, 0:n], ALU.add),
                                         reads=[greg, mreg], writes=[("merged", dc)])

            S.dma("sp", "lnbc", lnbc[:, 0:4096], lnv[l, 0:2].rearrange("a n -> (a n)").partition_broadcast(128), writes=["lnbc"])
            MREG = [("merged", c) for c in range(16)]
            for p in range(8):
                s, Wv = load_panel(w_out[l, p], 16, 256)
                for (t, R) in tiles_m:
                    b = bank()
                    for k in range(16):
                        S.op("pe", MM(ps[b][0:R, 0:256], merged[:, k, t * 128:t * 128 + R], Wv[:, k, :], k == 0, k == 15),
                             reads=[("w", s)] + MREG, writes=[("ps", b)])
                    xs = x[0:R, t, p * 256:(p + 1) * 256]
                    S.op("dve", STT(xs, xs, ALPHA, ps[b][0:R, 0:256], ALU.mult, ALU.add),
                         reads=[("ps", b), ("x", t)], writes=[("x", t)])
                    bn_piece(t, R, p, xs, ("x", t))
            ln_multi([(lambda o, n, t=t, R=R: x[0:R, t, o:o + n], R, ("x", t), t) for (t, R) in tiles_m], D, 0, D, pre=8)
            build_xT(tiles_m)
            for (t, R) in tiles_m:
                S.op("act", lambda e, t=t, R=R: e.mul(out=x[0:R, t, :], in_=x[0:R, t, :], mul=ALPHA),
                     reads=[("x", t)], writes=[("x", t)])

            S.dma("sp", "lnbc", lnbc[:, 0:4096], lnv[l, 2:4].rearrange("a n -> (a n)").partition_broadcast(128), writes=["lnbc"])
            fence(["meanb", "rstdb", "ptA", "ptB"], R_HT)
            def ffn_down(fb):
                hTb = hT[fb % 2]
                hreg = [("hT", fb % 2, 0), ("hT", fb % 2, 1)]
                s3, Wd = load_panel(w_down[l, fb], 2, 2048)
                for (t, R) in tiles_m:
                    for db in range(4):
                        b = bank()
                        for kc in range(2):
                            S.op("pe", MM(ps[b][0:R, 0:512], hTb[:, kc, t * 128:t * 128 + R], Wd[:, kc, db * 512:(db + 1) * 512],
                                          kc == 0, kc == 1),
                                 reads=[("w", s3)] + hreg, writes=[("ps", b)])
                        xs = x[0:R, t, db * 512:(db + 1) * 512]
                        S.op("dve", TT(xs, xs, ps[b][0:R, 0:512], ALU.add), reads=[("ps", b), ("x", t)], writes=[("x", t)])

            for fb in range(22):
                hTb = hT[fb % 2]
                hreg = [("hT", fb % 2, 0), ("hT", fb % 2, 1)]
                s1, Wgt = load_panel(w_up[l, fb], 16, 256)
                s2, Wup = load_panel(w_up[l, 22 + fb], 16, 256)
                for cc in range(2):
                    for (c0, n) in blocks_m:
                        bg = mm_fm(Wgt, s1, 16, cc, xT, TREG, c0, n)
                        gsb = gsig[gs_ctr % 3]
                        greg = ("gsig", gs_ctr % 3)
                        gs_ctr += 1
                        S.op("act", ACT(gsb[:, 0:n], ps[bg][:, 0:n], AF.Silu), reads=[("ps", bg)], writes=[greg])
                        bu = mm_fm(Wup, s2, 16, cc, xT, TREG, c0, n)
                        S.op("dve", TT(hTb[:, cc, c0:c0 + n], ps[bu][:, 0:n], gsb[:, 0:n], ALU.mult),
                             reads=[("ps", bu), greg], writes=[hreg[cc]])
                if fb >= 1:
                    ffn_down(fb - 1)
            ffn_down(21)
            ln_multi([(lambda o, n, t=t, R=R: x[0:R, t, o:o + n], R, ("x", t), t) for (t, R) in tiles_m], D, 0, D)

        def WB3(pv_, l, wcol):
            base = pv_[:, l, wcol:wcol + 30]
            return bass.AP(base.tensor, base.offset, [list(base.ap[0]), [0, 16], [1, 30]])

        for gi in range(2):
            g = groups[gi]
            for (t, R) in g["tiles"]:
                r0 = g["row0"] + t * 128
                S.dma("sp", "xin", x[0:R, t, :], xin[r0:r0 + R, :], writes=[("x", t)])
            for l in range(DEPTH):
                load_layer_consts(l)
                build_xT(g["tiles"])
                layer(gi, l)
            if gi == 0:
                for t in range(1, 5):
                    S.dma("sp", "o_y", yp[(t - 1) * 128:t * 128, :], x[:, t, :], reads=[("x", t)])
            else:
                for t in range(4):
                    S.dma("sp", "o_y", yp[512 + t * 128:512 + (t + 1) * 128, :], x[:, t, :], reads=[("x", t)])
                S.dma("sp", "o_y", ys, x[0:16, 4, :], reads=[("x", 4)])

        cnt = S.emit(final_waits=["o_y", "o_vs", "o_ps", "o_pp", "o_cs", "o_cp"])
    return nc


_NC = None


def _prep_inputs(inp):
    f = lambda a: np.ascontiguousarray(np.asarray(a, dtype=np.float32))
    xp = f(inp["x_prompt"])
    xs = f(inp["x_sample"])
    sp = f(inp["state_pool"])
    sc = f(inp["state_conv"])
    b_in = f(inp["b_in"])
    L = DEPTH
    pvec = np.zeros((L, 128, PV_N), np.float32)
    fm = lambda v: v.reshape(L, -1, 128).transpose(0, 2, 1)
    pvec[:, :, PV_BIN:PV_BIN + 88] = fm(b_in)
    pvec[:, :, PV_BSC:PV_BSC + 8] = fm(f(inp["b_scale"]))
    pvec[:, :, PV_CBD:PV_CBD + 8] = fm(f(inp["c_b_dw"]))
    pvec[:, :, PV_CLG:PV_CLG + 8] = fm(f(inp["c_ln_g"]))
    pvec[:, :, PV_CLB:PV_CLB + 8] = fm(f(inp["c_ln_b"]))
    wdw = f(inp["c_w_dw"])
    pvec[:, :, PV_WDW:PV_WDW + 248] = wdw.reshape(L, 31, 8, 128).transpose(0, 3, 2, 1).reshape(L, 128, 248)
    rowv = np.stack([f(inp["a_ln_g"]), f(inp["a_ln_b"]), b_in[:, 1024:2048]], axis=1)
    lnv = np.stack([f(inp["ln1_g"]), f(inp["ln1_b"]), f(inp["ln2_g"]), f(inp["ln2_b"])], axis=1)
    a_ws = f(inp["a_ws"])
    awsT = np.ascontiguousarray(a_ws.transpose(0, 3, 1, 2)).reshape(L, 128, 1024)
    a_bs = f(inp["a_bs"])
    absd = a_bs.reshape(L, 1024)
    aw00 = np.ascontiguousarray(a_ws[:, :, 0, 0]).reshape(L, 8)
    abs0 = np.ascontiguousarray(a_bs[:, :, 0]).reshape(L, 8)
    ident = np.eye(128, dtype=np.float32)
    causT = np.triu(np.ones((128, 128), np.float32))
    def tile_kn(w, K, NP):
        return np.ascontiguousarray(f(w).reshape(L, K, 128, NP, 256).transpose(0, 3, 2, 1, 4)).reshape(L, NP, 128, K * 256)

    w_down_t = np.ascontiguousarray(f(inp["w_ffn_down"]).reshape(L, 22, 2, 128, D).transpose(0, 1, 3, 2, 4)).reshape(L, 22, 128, 2 * D)
    b_wg_t = np.ascontiguousarray(f(inp["b_w_group"]).reshape(L, 4, 2, 128, 256).transpose(0, 3, 1, 2, 4)).reshape(L, 128, 2048)
    shared = {
        "ident": ident, "causT": causT,
        "w_in": tile_kn(inp["w_in"], 16, 44), "w_a_out": tile_kn(inp["w_a_out"], 8, 8),
        "w_b_out": tile_kn(inp["w_b_out"], 8, 8), "w_c_out": tile_kn(inp["w_c_out"], 8, 8),
        "w_out": tile_kn(inp["w_out"], 16, 8), "w_ffn_up": tile_kn(inp["w_ffn_up"], 16, 44), "w_ffn_down": w_down_t,
        "b_w_group": b_wg_t, "pvec": pvec, "rowv": np.ascontiguousarray(rowv), "lnv": np.ascontiguousarray(lnv),
        "awsT": awsT, "a_bs": np.ascontiguousarray(absd), "a_w00": aw00, "a_bs0": abs0,
    }
    in_maps = []
    for c in range(NCORES):
        seq, half = c // 2, c % 2
        xin = np.zeros((NTOK, D), np.float32)
        if half == 1:
            xin[0:128] = xp[seq, 896:1024]
        xin[128:1152] = xp[seq, half * 1024:(half + 1) * 1024]
        xin[1152:1168] = xs[c * 16:(c + 1) * 16, 0, :]
        mask = np.full((128, 1), float(half), np.float32)
        invc = np.zeros((128, 64), np.float32)
        for gq, w in enumerate(POOL_W):
            for tt in range(16):
                cntv = float(w) if half == 1 else float(min(tt + 1, w))
                invc[:, gq * 16 + tt] = 1.0 / cntv
        m = dict(shared)
        m.update({"xin": xin, "mask": mask, "invc": invc,
                  "st_pool": np.ascontiguousarray(sp[:, c * 16:(c + 1) * 16]),
                  "st_conv": np.ascontiguousarray(sc[:, c * 16:(c + 1) * 16])})
        in_maps.append(m)
    return in_maps


def kernel(**inputs):
    global _NC
    if _NC is None:
        _NC = build_program()
    in_maps = _prep_inputs(inputs)
    res = run_bass_kernel_spmd(_NC, in_maps, core_ids=list(range(NCORES)))
    R = res.results
    y_prompt = np.zeros((4, 2048, D), np.float32)
    y_sample = np.zeros((128, 1, D), np.float32)
    pool_p = np.zeros((DEPTH, 4, 15, 1024), np.float32)
    conv_p = np.zeros((DEPTH, 4, 30, 1024), np.float32)
    pool_s = np.zeros((DEPTH, 128, 15, 1024), np.float32)
    conv_s = np.zeros((DEPTH, 128, 30, 1024), np.float32)
    v_s = np.zeros((DEPTH, 128, 1, 1024), np.float32)
    for c in range(NCORES):
        seq, half = c // 2, c % 2
        r = R[c]
        y_prompt[seq, half * 1024:(half + 1) * 1024] = r["yp"]
        y_sample[c * 16:(c + 1) * 16, 0] = r["ys"]
        if half == 1:
            pool_p[:, seq] = r["o_pool_p"]
            conv_p[:, seq] = r["o_conv_p"]
        pool_s[:, c * 16:(c + 1) * 16] = r["o_pool_s"]
        conv_s[:, c * 16:(c + 1) * 16] = r["o_conv_s"]
        v_s[:, c * 16:(c + 1) * 16, 0] = r["o_v_s"]
    return (y_prompt, y_sample, pool_p, conv_p, pool_s, conv_s, v_s)
```

```python
import numpy as np
from contextlib import ExitStack
import concourse.bass as bass
import concourse.mybir as mybir
from concourse.bass_utils import run_bass_kernel_spmd

F32 = mybir.dt.float32
BF16 = mybir.dt.bfloat16
AF = mybir.ActivationFunctionType
ALU = mybir.AluOpType
AX = mybir.AxisListType

D = 2048
NIN = 11264
DFF = 5632
DEPTH = 2
ALPHA = (2.0 * DEPTH) ** 0.25
EPS = 1e-5
NCORES = 8
TA = 640
TB = 528
NTOK = TA + TB
POOL_W = (2, 4, 8, 16)
POOL_CONV_CH = (2, 5, 7)

ENGS = ("pe", "act", "dve", "pool", "sp")


class Op:
    __slots__ = ("eng", "fn", "deps", "need_inc", "cnt", "dma", "sem", "val")

    def __init__(self, eng, fn, dma=False):
        self.eng = eng
        self.fn = fn
        self.deps = []
        self.need_inc = False
        self.cnt = None
        self.dma = dma
        self.sem = None
        self.val = None


class Sched:
    def __init__(self, nc, stack):
        self.nc = nc
        self.stack = stack
        self.ops = {e: [] for e in ENGS}
        self.all_ops = []
        self.last_w = {}
        self.readers = {}
        self.eng_sem = {e: stack.enter_context(nc.semaphore("s_" + e)) for e in ENGS}
        self.dma_sems = {}
        self.dma_val = {}
        self.dma_closed = {}

    def _track(self, op, reads, writes):
        deps = set()
        for r in reads:
            w = self.last_w.get(r)
            if w is not None:
                deps.add(w)
        for w_ in writes:
            w = self.last_w.get(w_)
            if w is not None:
                deps.add(w)
            for rd in self.readers.get(w_, {}).values():
                deps.add(rd)
        deps.discard(op)
        op.deps = [("d", d.sem, self.dma_val[d.sem]) if d.dma else d for d in deps]
        for d in deps:
            if d.dma:
                self.dma_closed[d.sem] = self.dma_val[d.sem]
        rkey = ("d", op.sem) if op.dma else op.eng
        for r in reads:
            self.readers.setdefault(r, {})[rkey] = op
        for w_ in writes:
            self.last_w[w_] = op
            self.readers[w_] = {}

    def op(self, eng, fn, reads=(), writes=()):
        o = Op(eng, fn)
        self._track(o, reads, writes)
        self.ops[eng].append(o)
        self.all_ops.append(o)
        return o

    def dma(self, queue, sem, out, in_, reads=(), writes=(), **kw):
        if sem not in self.dma_sems:
            self.dma_sems[sem] = self.stack.enter_context(self.nc.semaphore("d_" + sem))
            self.dma_val[sem] = 0

        def fn(e, out=out, in_=in_, kw=kw):
            return e.dma_start(out=out, in_=in_, **kw)

        o = Op(queue, fn, dma=True)
        o.sem = sem
        self._track(o, reads, writes)
        if self.dma_closed.get(sem):
            o.deps.append(("d", sem, self.dma_closed[sem]))
            self.dma_closed[sem] = None
        self.dma_val[sem] += 16
        o.val = self.dma_val[sem]
        self.ops[queue].append(o)
        self.all_ops.append(o)
        return o

    def emit(self, final_waits):
        nc = self.nc
        for o in self.all_ops:
            for d in o.deps:
                if not isinstance(d, tuple):
                    if d.eng == "pe" and o.eng == "pe":
                        continue
                    d.need_inc = True
        cnt = {e: 0 for e in ENGS}
        for e in ENGS:
            for o in self.ops[e]:
                if not o.dma and o.need_inc:
                    cnt[e] += 1
                    o.cnt = cnt[e]
        sched = self

        def run(ename, e):
            waited = {}
            for o in sched.ops[ename]:
                need = {}
                for d in o.deps:
                    if isinstance(d, tuple):
                        key = ("d", d[1])
                        v = d[2]
                    else:
                        if d.eng == "pe" and ename == "pe":
                            continue
                        key = ("e", d.eng)
                        v = d.cnt
                    if v > need.get(key, 0):
                        need[key] = v
                for key, v in need.items():
                    if waited.get(key, 0) >= v:
                        continue
                    waited[key] = v
                    s = sched.dma_sems[key[1]] if key[0] == "d" else sched.eng_sem[key[1]]
                    e.wait_ge(s, v)
                ins = o.fn(e)
                if o.dma:
                    ins.then_inc(sched.dma_sems[o.sem], 16)
                elif o.need_inc:
                    ins.then_inc(sched.eng_sem[ename], 1)
            if ename == "sp":
                for sem in final_waits:
                    e.wait_ge(sched.dma_sems[sem], sched.dma_val[sem])

        with nc.Block() as block:
            @block.tensor
            def _(e):
                run("pe", e)

            @block.scalar
            def _(e):
                run("act", e)

            @block.vector
            def _(e):
                run("dve", e)

            @block.gpsimd
            def _(e):
                run("pool", e)

            @block.sync
            def _(e):
                run("sp", e)
        return cnt


def MM(out, lhsT, rhs, start, stop):
    return lambda e: e.matmul(out, lhsT=lhsT, rhs=rhs, start=start, stop=stop)


def TR(out, in_, ident):
    return lambda e: e.transpose(out, in_, ident)


def ACT(out, in_, func, bias=None, scale=None):
    kw = {}
    if bias is not None:
        kw["bias"] = bias
    if scale is not None:
        kw["scale"] = scale
    return lambda e: e.activation(out=out, in_=in_, func=func, **kw)


def ACP(out, in_):
    return lambda e: e.copy(out=out, in_=in_)


def TT(out, in0, in1, op):
    return lambda e: e.tensor_tensor(out=out, in0=in0, in1=in1, op=op)


def TS(out, in0, s1, s2, op0, op1=None):
    if op1 is None:
        return lambda e: e.tensor_scalar(out=out, in0=in0, scalar1=s1, scalar2=None, op0=op0)
    return lambda e: e.tensor_scalar(out=out, in0=in0, scalar1=s1, scalar2=s2, op0=op0, op1=op1)


def STT(out, in0, scalar, in1, op0, op1):
    return lambda e: e.scalar_tensor_tensor(out=out, in0=in0, scalar=scalar, in1=in1, op0=op0, op1=op1)


def CP(out, in_):
    return lambda e: e.tensor_copy(out=out, in_=in_)


def MSET(ap, v):
    return lambda e: e.memset(ap, v)


PV_BIN = 0
PV_BSC = 88
PV_CBD = 96
PV_CLG = 104
PV_CLB = 112
PV_WDW = 120
PV_N = 368


def build_program():
    nc = bass.Bass("TRN2", target_bir_lowering=False)

    def din(name, shape):
        return nc.dram_tensor(name, list(shape), F32, kind="ExternalInput").ap()

    def dout(name, shape):
        return nc.dram_tensor(name, list(shape), F32, kind="ExternalOutput").ap()

    xin = din("xin", [NTOK, D])
    maskd = din("mask", [128, 1])
    invcd = din("invc", [128, 64])
    identd = din("ident", [128, 128])
    causd = din("causT", [128, 128])
    st_pool = din("st_pool", [DEPTH, 16, 15, 1024])
    st_conv = din("st_conv", [DEPTH, 16, 30, 1024])
    w_in = din("w_in", [DEPTH, 44, 128, 4096])
    w_a_out = din("w_a_out", [DEPTH, 8, 128, 2048])
    w_b_out = din("w_b_out", [DEPTH, 8, 128, 2048])
    w_c_out = din("w_c_out", [DEPTH, 8, 128, 2048])
    w_out = din("w_out", [DEPTH, 8, 128, 4096])
    w_up = din("w_ffn_up", [DEPTH, 44, 128, 4096])
    w_down = din("w_ffn_down", [DEPTH, 22, 128, 4096])
    b_wg = din("b_w_group", [DEPTH, 128, 2048])
    pvec = din("pvec", [DEPTH, 128, PV_N])
    rowv = din("rowv", [DEPTH, 3, 1024])
    lnv = din("lnv", [DEPTH, 4, D])
    awsT = din("awsT", [DEPTH, 128, 8 * 128])
    absd = din("a_bs", [DEPTH, 8 * 128])
    aw00 = din("a_w00", [DEPTH, 8])
    abs0 = din("a_bs0", [DEPTH, 8])

    yp = dout("yp", [1024, D])
    ys = dout("ys", [16, D])
    o_pool_p = dout("o_pool_p", [DEPTH, 15, 1024])
    o_conv_p = dout("o_conv_p", [DEPTH, 30, 1024])
    o_pool_s = dout("o_pool_s", [DEPTH, 16, 15, 1024])
    o_conv_s = dout("o_conv_s", [DEPTH, 16, 30, 1024])
    o_v_s = dout("o_v_s", [DEPTH, 16, 1024])

    with ExitStack() as st:
        S = Sched(nc, st)

        def sb(name, shape, dt=F32):
            return st.enter_context(nc.sbuf_tensor(name, list(shape), dt))

        x = sb("x", [128, 5, D])
        xT = sb("xT", [128, 16, TA], BF16)
        uT = sb("uT", [128, 8, TA], BF16)
        bmix = sb("bmix", [128, 8, TA], BF16)
        cact = sb("cact", [128, 8, TA], BF16)
        arena = sb("arena", [128, 10480])
        scr = sb("scr", [128, 1310])
        fscr = sb("fscr", [128, 2])
        epsT = sb("epsT", [128, 1])
        lnbc = sb("lnbc", [128, 4096])
        wsl = [sb(f"wsl{i}", [128, 4096], BF16) for i in range(4)]
        pv = sb("pv", [128, DEPTH, PV_N])
        ident = sb("ident_s", [128, 128])
        identb = sb("identb", [128, 128], BF16)
        ones32 = sb("ones32", [128, 128])
        wmT = sb("wmT", [128, 8, 128], BF16)
        wdg = sb("wdg", [128, 8, 128], BF16)
        w00 = sb("w00", [128, 8])
        bs0 = sb("bs0", [128, 8])
        mask = sb("mask_s", [128, 1])
        invc = sb("invc_s", [128, 64])
        pprefix = sb("pprefix", [128, DEPTH, 8, 15])
        cprefix = sb("cprefix", [128, DEPTH, 8, 30])
        stats = [sb(f"stats{i}", [128, 32]) for i in range(5)]
        tmpA = [sb(f"tmpA{i}", [128, 512]) for i in range(2)]
        sq = [sb(f"sq{i}", [128, 320]) for i in range(2)]
        ptA = scr[:, 0:655]
        ptB = scr[:, 655:1310]
        meanb = scr[:, 0:640]
        rstdb = scr[:, 640:1280]
        hT = [scr.bitcast(BF16)[:, i * 1280:(i + 1) * 1280].rearrange("p (a b) -> p a b", b=TA) for i in range(2)]
        gsig = [sb(f"gsig{i}", [128, 320]) for i in range(3)]
        awf = lnbc[:, 0:1024]
        caus = lnbc[:, 1024:1152]
        bsbc = lnbc[:, 3072:4096].rearrange("p (h i) -> p h i", i=128)

        def RG(name, n):
            return [(name, i) for i in range(n)]

        R_XB = RG("xb16", 2)
        R_HT = [("hT", i, c) for i in range(2) for c in range(2)]

        def fence(old, new):
            S.op("dve", MSET(fscr[:, 0:1], 0.0), writes=list(old) + list(new))
        ps = [st.enter_context(nc.psum_tensor(f"ps{i}", [128, 512], F32)) for i in range(8)]
        psb = [p.bitcast(BF16) for p in ps]

        def aview(off, shape, dt=F32):
            n = 1
            for s_ in shape[1:]:
                n *= s_
            if dt == F32:
                ap = arena[:, off:off + n]
            else:
                ap = arena.bitcast(BF16)[:, 2 * off:2 * off + n]
            if len(shape) == 3:
                ap = ap.rearrange("p (a b) -> p a b", b=shape[2])
            return ap

        vraw = aview(0, [128, 5, 1024])
        vbf = aview(5120, [128, 5, 1024], BF16)
        xb16 = [aview(8000 + i * 1024, [128, 2048], BF16) for i in range(2)]
        hbx = aview(0, [128, 8, 15 + TA])
        pooled = aview(5240, [128, 8, TA], BF16)
        glxb = aview(0, [128, 8, 30 + TA], BF16)
        gtail = aview(2680, [128, 8, 46])
        dgb = aview(3048, [128, 31, 128], BF16)
        conv = aview(5360, [128, 8, TA])
        merged = aview(0, [128, 16, TA], BF16)
        macc = aview(5120, [128, 2, TA])
        groups = [
            dict(T=TA, tiles=[(0, 128), (1, 128), (2, 128), (3, 128), (4, 128)], blocks=[(0, 320), (320, 320)],
                 row0=0, sample=False),
            dict(T=TB, tiles=[(0, 128), (1, 128), (2, 128), (3, 128), (4, 16)], blocks=[(0, 264), (264, 264)],
                 row0=TA, sample=True),
        ]

        bank_ctr = [0]

        def bank():
            b = bank_ctr[0] % 8
            bank_ctr[0] += 1
            return b

        slot_ctr = [0]

        def load_panel(src_ap, K, C):
            s = slot_ctr[0] % 4
            slot_ctr[0] += 1
            S.dma("pool", f"w{s}", wsl[s][:, 0:K * C], src_ap, writes=[("w", s)])
            return s, wsl[s][:, 0:K * C].rearrange("p (k c) -> p k c", c=C)

        XT_ALL = [("xT", t) for t in range(5)]

        S.dma("sp", "cst", ident[:], identd, writes=["ident"])
        S.dma("sp", "cst", mask[:], maskd, writes=["mask"])
        S.dma("sp", "cst", invc[:], invcd, writes=["invc"])
        S.dma("sp", "cst", pv[:], pvec.rearrange("l p n -> p l n"), writes=["pv"])
        S.op("dve", CP(identb[:], ident[:]), reads=["ident"], writes=["identb"])
        S.op("dve", MSET(ones32[:], 1.0), writes=["ones32"])
        S.op("dve", MSET(epsT[:], EPS), writes=["epsT"])

        def PV(l, col, n=1):
            return pv[:, l, col:col + n]

        def load_layer_consts(l):
            S.dma("sp", "lnbc", awf, awsT[l], writes=["lnbc"])
            S.dma("sp", "lnbc", caus, causd, writes=[])
            S.dma("sp", "cst", w00[:], aw00[l].partition_broadcast(128), writes=["w00"])
            S.dma("sp", "cst", bs0[:], abs0[l].partition_broadcast(128), writes=["bs0"])
            for h in range(8):
                S.op("dve", TT(wmT[:, h, :], awf[:, h * 128:(h + 1) * 128], caus, ALU.mult),
                     reads=["lnbc"], writes=["wmT"])
                S.op("dve", TS(wdg[:, h, :], ident[:], w00[:, h:h + 1], None, ALU.mult),
                     reads=["ident", "w00"], writes=["wdg"])

        def build_xT(tl):
            for i, (t, R) in enumerate(tl):
                xb = xb16[i % 2]
                S.op("act", ACP(xb[0:R, :], x[0:R, t, :]), reads=[("x", t)], writes=[("xb16", i % 2)])
                for half in range(2):
                    b = bank()
                    for kk in range(8):
                        k = half * 8 + kk
                        S.op("pe", TR(psb[b][:, kk * 128:kk * 128 + R], xb[0:R, k * 128:(k + 1) * 128], identb[0:R, 0:R]),
                             reads=[("xb16", i % 2), "identb"], writes=[("ps", b)])
                    src = psb[b][:, 0:1024].rearrange("p (k c) -> p k c", c=128)[:, :, 0:R]
                    S.op("dve", CP(xT[:, half * 8:half * 8 + 8, t * 128:t * 128 + R], src),
                         reads=[("ps", b)], writes=[("xT", t)])

        def mm_fm(Wv, s, K, cc, src, src_regs, c0, n):
            if src_regs and src_regs[0][0] == "xT":
                src_regs = [("xT", t) for t in range(c0 // 128, (c0 + n - 1) // 128 + 1)]
            b = bank()
            for k in range(K):
                S.op("pe", MM(ps[b][:, 0:n], Wv[:, k, cc * 128:(cc + 1) * 128], src[:, k, c0:c0 + n], k == 0, k == K - 1),
                     reads=[("w", s)] + src_regs, writes=[("ps", b)])
            return b

        def ln_multi(items, Dn, gcol, bcol):
            for _ in ln_stages(items, Dn, gcol, bcol):
                pass

        def ln_stages(items, Dn, gcol, bcol):
            nch = Dn // 512
            its = []
            for i, (ap_fn, R, reg) in enumerate(items):
                its.append((ap_fn, R, reg, stats[i], ("st", i)))
            for (ap_fn, R, reg, stt, sreg) in its:
                for c in range(nch):
                    S.op("dve", lambda e, c=c, stt=stt, R=R, ap_fn=ap_fn: e.bn_stats(out=stt[0:R, c * 6:(c + 1) * 6], in_=ap_fn(c * 512, 512)),
                         reads=[reg], writes=[sreg])
                S.op("dve", lambda e, stt=stt, R=R: e.bn_aggr(out=stt[0:R, 24:26], in_=stt[0:R, 0:nch * 6]), reads=[sreg], writes=[sreg])
            yield 0
            for (ap_fn, R, reg, stt, sreg) in its:
                S.op("act", ACT(stt[0:R, 26:27], stt[0:R, 25:26], AF.Sqrt, bias=epsT[0:R, 0:1], scale=1.0),
                     reads=[sreg, "epsT"], writes=[sreg])
            yield 1
            for (ap_fn, R, reg, stt, sreg) in its:
                S.op("dve", lambda e, stt=stt, R=R: e.reciprocal(out=stt[0:R, 26:27], in_=stt[0:R, 26:27]), reads=[sreg], writes=[sreg])
                S.op("dve", STT(stt[0:R, 27:28], stt[0:R, 24:25], -1.0, stt[0:R, 26:27], ALU.mult, ALU.mult),
                     reads=[sreg], writes=[sreg])
            yield 2
            for (ap_fn, R, reg, stt, sreg) in its:
                full = ap_fn(0, Dn)
                S.op("act", ACT(full, full, AF.Identity, bias=stt[0:R, 27:28], scale=stt[0:R, 26:27]),
                     reads=[reg, sreg], writes=[reg])
            yield 3
            for (ap_fn, R, reg, stt, sreg) in its:
                full = ap_fn(0, Dn)
                S.op("dve", TT(full, full, lnbc[0:R, gcol:gcol + Dn], ALU.mult), reads=[reg, "lnbc"], writes=[reg])
            yield 4
            for (ap_fn, R, reg, stt, sreg) in its:
                full = ap_fn(0, Dn)
                S.op("dve", TT(full, full, lnbc[0:R, bcol:bcol + Dn], ALU.add), reads=[reg, "lnbc"], writes=[reg])
            yield 5

        def layer(gi, l):
            g = groups[gi]
            T = g["T"]
            tiles = g["tiles"]
            blocks = g["blocks"]
            sample = g["sample"]
            TREG = [("xT", t) for t, _ in tiles]
            light = (gi == 0 and l == DEPTH - 1)
            tiles_m = tiles[1:] if light else tiles
            blocks_m = [(128, 256), (384, 256)] if light else blocks
            Tm0 = 128 if light else 0

            def au_panel(p):
                s, Wv = load_panel(w_in[l, p], 16, 256)
                for cc in range(2):
                    ch = p * 2 + cc
                    for (c0, n) in blocks_m:
                        b = mm_fm(Wv, s, 16, cc, xT, TREG, c0, n)
                        S.op("act", ACT(uT[:, ch, c0:c0 + n], ps[b][:, 0:n], AF.Gelu, bias=PV(l, PV_BIN + ch), scale=1.0),
                             reads=[("ps", b), "pv"], writes=[("uT", ch)])

            fence(RG("merged", 16) + RG("macc", 2) + R_XB, RG("vraw", 5) + RG("vbf", 5))
            S.dma("sp", "lnbc", lnbc[:, 0:3072], rowv[l].rearrange("a n -> (a n)").partition_broadcast(128), writes=["lnbc"])
            S.dma("sp", "lnbc", lnbc[:, 3072:4096], absd[l].partition_broadcast(128), writes=[])
            for p in range(4):
                s, Wv = load_panel(w_in[l, 4 + p], 16, 256)
                for (t, R) in tiles_m:
                    b = bank()
                    for k in range(16):
                        S.op("pe", MM(ps[b][0:R, 0:256], xT[:, k, t * 128:t * 128 + R], Wv[:, k, :], k == 0, k == 15),
                             reads=[("w", s), ("xT", t)], writes=[("ps", b)])
                    S.op("dve", TT(vraw[0:R, t, p * 256:(p + 1) * 256], ps[b][0:R, 0:256],
                                   lnbc[0:R, 2048 + p * 256:2048 + (p + 1) * 256], ALU.add),
                         reads=[("ps", b), "lnbc"], writes=[("vraw", t)])
            for (t, R) in tiles_m:
                S.op("act", ACT(vraw[0:R, t, :], vraw[0:R, t, :], AF.Gelu), reads=[("vraw", t)], writes=[("vraw", t)])
            au_next = 0
            for stg in ln_stages([(lambda o, n, t=t, R=R: vraw[0:R, t, o:o + n], R, ("vraw", t)) for (t, R) in tiles_m],
                                 1024, 0, 1024):
                if stg in (0, 2, 3, 5):
                    au_panel(au_next)
                    au_next += 1
            for (t, R) in tiles_m:
                S.op("act", ACP(vbf[0:R, t, :], vraw[0:R, t, :]), reads=[("vraw", t)], writes=[("vbf", t)])
            if sample:
                S.dma("sp", "o_vs", o_v_s[l], vraw[0:16, 4, :], reads=[("vraw", 4)])
            for (t, R) in tiles_m:
                is_s = sample and t == 4
                for hh in range(2):
                    b = bank()
                    for h4 in range(4):
                        h = hh * 4 + h4
                        rhs = wdg[0:R, h, 0:R] if is_s else wmT[:, h, :]
                        S.op("pe", MM(ps[b][:, h4 * 128:h4 * 128 + R], vbf[0:R, t, h * 128:(h + 1) * 128], rhs, True, True),
                             reads=[("vbf", t), "wdg" if is_s else "wmT"], writes=[("ps", b)])
                    tm = tmpA[hh]
                    pv3 = ps[b][:, 0:512].rearrange("p (h c) -> p h c", c=128)[:, :, 0:R]
                    tm3 = tm[:, 0:512].rearrange("p (h c) -> p h c", c=128)[:, :, 0:R]
                    bsrc = bsbc[:, hh * 4:hh * 4 + 4, 0:R]
                    if is_s:
                        for h4 in range(4):
                            h = hh * 4 + h4
                            S.op("dve", TS(tm[:, h4 * 128:h4 * 128 + R], ps[b][:, h4 * 128:h4 * 128 + R], bs0[:, h:h + 1], None, ALU.add),
                                 reads=[("ps", b), "bs0"], writes=[("tmpA", hh)])
                    else:
                        S.op("dve", TT(tm3, pv3, bsrc, ALU.add), reads=[("ps", b), "lnbc"], writes=[("tmpA", hh)])
                    dst = uT[:, hh * 4:hh * 4 + 4, t * 128:t * 128 + R]
                    S.op("dve", TT(dst, tm3, dst, ALU.mult), reads=[("tmpA", hh)] + [("uT", hh * 4 + i) for i in range(4)],
                         writes=[("uT", hh * 4 + i) for i in range(4)])

            fence(RG("vraw", 5) + RG("vbf", 5) + R_HT, RG("hbx", 8) + RG("pooled", 8) + ["ptA", "ptB"])
            if gi == 0:
                S.op("dve", MSET(hbx[:, :, 0:15], 0.0), reads=[], writes=[("hbx", c) for c in range(8)])
            else:
                S.op("dve", CP(hbx[:, :, 0:15], pprefix[:, l, :, :]), reads=[("pprefix", l)],
                     writes=[("hbx", c) for c in range(8)])
            if sample:
                S.dma("sp", "lnbc", lnbc[0:120, 0:2048].rearrange("p (a n) -> p a n", a=2),
                      st_pool[l].rearrange("(a b) k n -> (b k) a n", a=2), writes=["lnbc"])
                S.dma("sp", "o_ps", o_pool_s[l][:, 0:14, :], st_pool[l][:, 1:15, :])
            for p in range(4):
                s, Wv = load_panel(w_in[l, 8 + p], 16, 256)
                for cc in range(2):
                    ch = p * 2 + cc
                    for (c0, n) in blocks:
                        b = mm_fm(Wv, s, 16, cc, xT, TREG, c0, n)
                        S.op("act", ACT(hbx[:, ch, 15 + c0:15 + c0 + n], ps[b][:, 0:n], AF.Identity,
                                        bias=PV(l, PV_BIN + 16 + ch), scale=1.0),
                             reads=[("ps", b), "pv"], writes=[("hbx", ch)])
            sg = slot_ctr[0] % 4
            slot_ctr[0] += 1
            WG = wsl[sg][:, 0:2048].rearrange("p (g c d) -> p g c d", g=4, c=2)
            S.dma("pool", f"w{sg}", wsl[sg][:, 0:2048], b_wg[l], writes=[("w", sg)])
            for ch in range(8):
                hreg = ("hbx", ch)
                if gi == 0:
                    S.op("dve", TS(hbx[:, ch, 15:15 + 128], hbx[:, ch, 15:15 + 128], mask[:, 0:1], None, ALU.mult),
                         reads=[hreg, "mask"], writes=[hreg])
                gq = ch // 2
                w = POOL_W[gq]
                ext = hbx[:, ch, :]
                L = 15 + T
                cur = ext
                off = 0
                steps = [(1, ptA), (2, ptB), (4, ptA), (8, ptB)][:gq + 1]
                for (sh, dstbuf) in steps:
                    no = off + sh
                    S.op("dve", TT(dstbuf[:, no:L], cur[:, no:L], cur[:, off:L - sh], ALU.add),
                         reads=[hreg, "ptA", "ptB"], writes=["ptA" if dstbuf is ptA else "ptB"])
                    cur = dstbuf
                    off = no
                S.op("dve", STT(pooled[:, ch, 0:T], cur[:, 15:15 + T], 1.0 / w, ext[:, 15:15 + T], ALU.mult, ALU.subtract),
                     reads=[hreg, "ptA", "ptB"], writes=[("pooled", ch)])
                if gi == 0:
                    S.op("dve", TT(tmpA[0][:, 0:16], cur[:, 15 + 128:15 + 144], invc[:, gq * 16:gq * 16 + 16], ALU.mult),
                         reads=["ptA", "ptB", "invc"], writes=[("tmpA", 0)])
                    S.op("dve", TT(pooled[:, ch, 128:144], tmpA[0][:, 0:16], ext[:, 15 + 128:15 + 144], ALU.subtract),
                         reads=[("tmpA", 0), hreg], writes=[("pooled", ch)])
                    S.op("dve", CP(pprefix[:, l, ch, :], ext[:, T:T + 15]), reads=[hreg], writes=[("pprefix", l)])
                if sample:
                    b = bank()
                    for a in range(2):
                        S.op("pe", TR(ps[b][:, a * 120:(a + 1) * 120], lnbc[0:120, a * 1024 + ch * 128:a * 1024 + (ch + 1) * 128],
                                      ident[0:120, 0:120]), reads=["lnbc", "ident"], writes=[("ps", b)])
                    stv = ps[b][:, 0:240].rearrange("p (b k) -> p b k", k=15)
                    hnew = hbx[:, ch, 15 + 512:15 + 528]
                    if w > 2:
                        S.op("dve", lambda e, stv=stv, w=w: e.reduce_sum(out=tmpA[1][:, 0:16], in_=stv[:, :, 16 - w:15], axis=AX.X),
                             reads=[("ps", b)], writes=[("tmpA", 1)])
                    else:
                        S.op("dve", CP(tmpA[1][:, 0:16], stv[:, :, 14]), reads=[("ps", b)], writes=[("tmpA", 1)])
                    S.op("dve", TT(tmpA[1][:, 0:16], tmpA[1][:, 0:16], hnew, ALU.add), reads=[("tmpA", 1), hreg], writes=[("tmpA", 1)])
                    S.op("dve", STT(pooled[:, ch, 512:528], tmpA[1][:, 0:16], 1.0 / w, hnew, ALU.mult, ALU.subtract),
                         reads=[("tmpA", 1), hreg], writes=[("pooled", ch)])
            if sample:
                b = bank()
                b2 = bank()
                for ch in range(8):
                    bb = b if ch < 4 else b2
                    S.op("pe", TR(ps[bb][0:15, (ch % 4) * 128:(ch % 4 + 1) * 128], hbx[:, ch, 15 + 497:15 + 512], ident[:, :]),
                         reads=[("hbx", ch), "ident"], writes=[("ps", bb)])
                S.op("dve", CP(tmpA[0][0:15, 0:512], ps[b][0:15, 0:512]), reads=[("ps", b)], writes=[("tmpA", 0)])
                S.op("dve", CP(tmpA[1][0:15, 0:512], ps[b2][0:15, 0:512]), reads=[("ps", b2)], writes=[("tmpA", 1)])
                S.dma("sp", "o_pp", o_pool_p[l][:, 0:512], tmpA[0][0:15, 0:512], reads=[("tmpA", 0)])
                S.dma("sp", "o_pp", o_pool_p[l][:, 512:1024], tmpA[1][0:15, 0:512], reads=[("tmpA", 1)])
                b = bank()
                b2 = bank()
                for ch in range(8):
                    bb = b if ch < 4 else b2
                    S.op("pe", TR(ps[bb][0:16, (ch % 4) * 128:(ch % 4 + 1) * 128], hbx[:, ch, 15 + 512:15 + 528], ident[:, :]),
                         reads=[("hbx", ch), "ident"], writes=[("ps", bb)])
                S.op("dve", CP(tmpA[0][0:16, 0:512], ps[b][0:16, 0:512]), reads=[("ps", b)], writes=[("tmpA", 0)])
                S.op("dve", CP(tmpA[1][0:16, 0:512], ps[b2][0:16, 0:512]), reads=[("ps", b2)], writes=[("tmpA", 1)])
                S.dma("sp", "o_ps", o_pool_s[l][:, 14, 0:512], tmpA[0][0:16, 0:512], reads=[("tmpA", 0)])
                S.dma("sp", "o_ps", o_pool_s[l][:, 14, 512:1024], tmpA[1][0:16, 0:512], reads=[("tmpA", 1)])
            for gq in range(4):
                for dd in range(2):
                    ch = gq * 2 + dd
                    for (c0, n) in blocks_m:
                        b = bank()
                        for kc in range(2):
                            S.op("pe", MM(ps[b][:, 0:n], WG[:, gq, kc, dd * 128:(dd + 1) * 128], pooled[:, gq * 2 + kc, c0:c0 + n],
                                          kc == 0, kc == 1),
                                 reads=[("w", sg), ("pooled", gq * 2), ("pooled", gq * 2 + 1)], writes=[("ps", b)])
                        S.op("act", ACT(bmix[:, ch, c0:c0 + n], ps[b][:, 0:n], AF.Identity, scale=PV(l, PV_BSC + ch)),
                             reads=[("ps", b), "pv"], writes=[("bmix", ch)])

            fence(RG("hbx", 8) + RG("pooled", 8) + ["ptA", "ptB"] + R_XB,
                  RG("glxb", 8) + RG("gtail", 8) + ["dg"] + RG("conv", 8) + ["meanb", "rstdb"])
            if gi == 0:
                S.op("dve", MSET(glxb[:, :, 0:30], 0.0), reads=[], writes=RG("glxb", 8))
            else:
                S.op("dve", CP(glxb[:, :, 0:30], cprefix[:, l, :, :]), reads=[("cprefix", l)], writes=RG("glxb", 8))
            if sample:
                S.dma("sp", "lnbc", lnbc[0:120, 0:4096].rearrange("p (a n) -> p a n", a=4),
                      st_conv[l].rearrange("(a b) k n -> (b k) a n", a=4), writes=["lnbc"])
                S.dma("sp", "o_cs", o_conv_s[l][:, 0:29, :], st_conv[l][:, 1:30, :])
            ntail = 46 if sample else 30
            tail_lo = (482 - blocks[1][0]) if sample else (blocks[1][1] - 30)
            gs_ctr = 0
            idb_bc = bass.AP(identb[:, :].tensor, identb[:, :].offset, [list(identb[:, :].ap[0]), [0, 31], [1, 128]])

            def conv_pe(ch):
                wcol = PV_WDW + ch * 31
                for (c0, n) in blocks_m:
                    b = bank()
                    for k in range(31):
                        S.op("pe", MM(ps[b][:, 0:n], dgb[:, k, :], glxb[:, ch, c0 + k:c0 + k + n], k == 0, k == 30),
                             reads=["dg", ("glxb", ch)], writes=[("ps", b)])
                    S.op("act", ACT(conv[:, ch, c0:c0 + n], ps[b][:, 0:n], AF.Identity, bias=PV(l, PV_CBD + ch), scale=1.0),
                         reads=[("ps", b), "pv"], writes=[("conv", ch)])
                if sample:
                    gnew = gtail[:, ch, 30:46]
                    b = bank()
                    for a in range(4):
                        S.op("pe", TR(ps[b][:, a * 120:(a + 1) * 120],
                                      lnbc[0:120, a * 1024 + ch * 128:a * 1024 + (ch + 1) * 128], ident[0:120, 0:120]),
                             reads=["lnbc", "ident"], writes=[("ps", b)])
                    stv = ps[b][:, 0:480].rearrange("p (b k) -> p b k", k=30)
                    tm3 = tmpA[0][:, 0:480].rearrange("p (b k) -> p b k", k=30)
                    S.op("dve", TT(tm3, stv, WB3(pv, l, wcol), ALU.mult), reads=[("ps", b), "pv"], writes=[("tmpA", 0)])
                    S.op("dve", lambda e, tm3=tm3: e.reduce_sum(out=tmpA[1][:, 0:16], in_=tm3, axis=AX.X),
                         reads=[("tmpA", 0)], writes=[("tmpA", 1)])
                    S.op("dve", TS(tmpA[1][:, 16:32], gnew, PV(l, wcol + 30), PV(l, PV_CBD + ch), ALU.mult, ALU.add),
                         reads=[("gtail", ch), "pv"], writes=[("tmpA", 1)])
                    S.op("dve", TT(conv[:, ch, 512:528], tmpA[1][:, 0:16], tmpA[1][:, 16:32], ALU.add),
                         reads=[("tmpA", 1)], writes=[("conv", ch)])

            for ch in range(8):
                p, cc = ch // 2, ch % 2
                if cc == 0:
                    sg_, Wgt_ = load_panel(w_in[l, 16 + p], 16, 256)
                    sv_, Wvl_ = load_panel(w_in[l, 12 + p], 16, 256)
                for bi_, (c0, n) in enumerate(blocks):
                    bg = mm_fm(Wgt_, sg_, 16, cc, xT, TREG, c0, n)
                    gsb = gsig[gs_ctr % 3]
                    greg = ("gsig", gs_ctr % 3)
                    gs_ctr += 1
                    S.op("act", ACT(gsb[:, 0:n], ps[bg][:, 0:n], AF.Sigmoid, bias=PV(l, PV_BIN + 32 + ch), scale=1.0),
                         reads=[("ps", bg), "pv"], writes=[greg])
                    bv = mm_fm(Wvl_, sv_, 16, cc, xT, TREG, c0, n)
                    S.op("dve", STT(glxb[:, ch, 30 + c0:30 + c0 + n], ps[bv][:, 0:n], PV(l, PV_BIN + 24 + ch), gsb[:, 0:n],
                                    ALU.add, ALU.mult),
                         reads=[("ps", bv), greg, "pv"], writes=[("glxb", ch)])
                    if bi_ == 1:
                        S.op("dve", STT(gtail[:, ch, 0:ntail], ps[bv][:, tail_lo:tail_lo + ntail], PV(l, PV_BIN + 24 + ch),
                                        gsb[:, tail_lo:tail_lo + ntail], ALU.add, ALU.mult),
                             reads=[("ps", bv), greg, "pv"], writes=[("gtail", ch)])
                if gi == 0:
                    S.op("dve", TS(glxb[:, ch, 30:30 + 128], glxb[:, ch, 30:30 + 128], mask[:, 0:1], None, ALU.mult),
                         reads=[("glxb", ch), "mask"], writes=[("glxb", ch)])
                    S.op("dve", CP(cprefix[:, l, ch, :], gtail[:, ch, 0:30]), reads=[("gtail", ch)], writes=[("cprefix", l)])
                if ch >= 1:
                    conv_pe(ch - 1)
                wcol = PV_WDW + ch * 31
                wb_ = pv[:, l, wcol:wcol + 31]
                w_bc = bass.AP(wb_.tensor, wb_.offset, [list(wb_.ap[0]), [1, 31], [0, 128]])
                S.op("dve", TT(dgb, idb_bc, w_bc, ALU.mult), reads=["identb", "pv"], writes=["dg"])
            conv_pe(7)
            if sample:
                for (n_r, c_lo, dst_ap, semn) in ((30, 0, o_conv_p[l], "o_cp"), (16, 30, o_conv_s[l][:, 29, :], "o_cs")):
                    b = bank()
                    b2 = bank()
                    for ch in range(8):
                        bb = b if ch < 4 else b2
                        S.op("pe", TR(ps[bb][0:n_r, (ch % 4) * 128:(ch % 4 + 1) * 128], gtail[:, ch, c_lo:c_lo + n_r], ident[:, :]),
                             reads=[("gtail", ch), "ident"], writes=[("ps", bb)])
                    S.op("dve", CP(tmpA[0][0:n_r, 0:512], ps[b][0:n_r, 0:512]), reads=[("ps", b)], writes=[("tmpA", 0)])
                    S.op("dve", CP(tmpA[1][0:n_r, 0:512], ps[b2][0:n_r, 0:512]), reads=[("ps", b2)], writes=[("tmpA", 1)])
                    S.dma("sp", semn, dst_ap[:, 0:512], tmpA[0][0:n_r, 0:512], reads=[("tmpA", 0)])
                    S.dma("sp", semn, dst_ap[:, 512:1024], tmpA[1][0:n_r, 0:512], reads=[("tmpA", 1)])
            CONV_ALL = [("conv", c) for c in range(8)]
            sq_ctr = 0
            for (c0, n) in blocks_m:
                b1 = bank()
                for ch in range(8):
                    S.op("pe", MM(ps[b1][:, 0:n], ones32[:, :], conv[:, ch, c0:c0 + n], ch == 0, ch == 7),
                         reads=[("conv", ch), "ones32"], writes=[("ps", b1)])
                b2 = bank()
                for ch in range(8):
                    sqb = sq[sq_ctr % 2]
                    sreg = ("sq", sq_ctr % 2)
                    sq_ctr += 1
                    S.op("act", ACT(sqb[:, 0:n], conv[:, ch, c0:c0 + n], AF.Square), reads=[("conv", ch)], writes=[sreg])
                    S.op("pe", MM(ps[b2][:, 0:n], ones32[:, :], sqb[:, 0:n], ch == 0, ch == 7),
                         reads=[sreg, "ones32"], writes=[("ps", b2)])
                mb = meanb[:, c0:c0 + n]
                rb = rstdb[:, c0:c0 + n]
                S.op("dve", TS(mb, ps[b1][:, 0:n], 1.0 / 1024, None, ALU.mult), reads=[("ps", b1)], writes=["meanb"])
                S.op("dve", TT(tmpA[0][:, 0:n], mb, mb, ALU.mult), reads=["meanb"], writes=[("tmpA", 0)])
                S.op("dve", STT(rb, ps[b2][:, 0:n], 1.0 / 1024, tmpA[0][:, 0:n], ALU.mult, ALU.subtract),
                     reads=[("ps", b2), ("tmpA", 0)], writes=["rstdb"])
                S.op("dve", TS(rb, rb, 0.0, None, ALU.max), reads=["rstdb"], writes=["rstdb"])
                S.op("act", ACT(rb, rb, AF.Sqrt, bias=epsT[:, 0:1], scale=1.0), reads=["rstdb", "epsT"], writes=["rstdb"])
                S.op("dve", lambda e, rb=rb: e.reciprocal(out=rb, in_=rb), reads=["rstdb"], writes=["rstdb"])
            for ch in range(8):
                cv = conv[:, ch, Tm0:T]
                S.op("dve", TT(cv, cv, meanb[:, Tm0:T], ALU.subtract), reads=[("conv", ch), "meanb"], writes=[("conv", ch)])
                S.op("dve", TT(cv, cv, rstdb[:, Tm0:T], ALU.mult), reads=[("conv", ch), "rstdb"], writes=[("conv", ch)])
                S.op("act", ACT(cact[:, ch, Tm0:T], cv, AF.Silu, bias=PV(l, PV_CLB + ch), scale=PV(l, PV_CLG + ch)),
                     reads=[("conv", ch), "pv"], writes=[("cact", ch)])

            branches = [(w_a_out, uT, [("uT", c) for c in range(8)]),
                        (w_b_out, bmix, [("bmix", c) for c in range(8)]),
                        (w_c_out, cact, [("cact", c) for c in range(8)])]
            fence(RG("glxb", 8) + RG("gtail", 8) + ["dg"] + RG("conv", 8), RG("merged", 16) + RG("macc", 2) + R_XB)
            gs_ctr = 0
            for dp in range(8):
                for bi, (wbr, src, sregs) in enumerate(branches):
                    col = 5120 + bi * 2048 + dp * 256
                    s1, Wg = load_panel(w_in[l, col // 256], 16, 256)
                    s2, Wo = load_panel(wbr[l, dp], 8, 256)
                    for cc in range(2):
                        dc = dp * 2 + cc
                        for (c0, n) in blocks_m:
                            bg = mm_fm(Wg, s1, 16, cc, xT, TREG, c0, n)
                            gsb = gsig[gs_ctr % 3]
                            greg = ("gsig", gs_ctr % 3)
                            gs_ctr += 1
                            S.op("act", ACT(gsb[:, 0:n], ps[bg][:, 0:n], AF.Sigmoid,
                                            bias=PV(l, PV_BIN + 40 + bi * 16 + dc), scale=1.0),
                                 reads=[("ps", bg), "pv"], writes=[greg])
                            by = mm_fm(Wo, s2, 8, cc, src, sregs, c0, n)
                            mreg = ("macc", cc)
                            ma = macc[:, cc, c0:c0 + n]
                            if bi == 0:
                                S.op("dve", TT(ma, ps[by][:, 0:n], gsb[:, 0:n], ALU.mult),
                                     reads=[("ps", by), greg], writes=[mreg])
                            else:
                                S.op("dve", TT(gsb[:, 0:n], ps[by][:, 0:n], gsb[:, 0:n], ALU.mult),
                                     reads=[("ps", by), greg], writes=[greg])
                                if bi == 1:
                                    S.op("dve", TT(ma, ma, gsb[:, 0:n], ALU.add), reads=[greg, mreg], writes=[mreg])
                                else:
                                    S.op("dve", TT(merged[:, dc, c0:c0 + n], ma, gsb[:, 0:n], ALU.add),
                                         reads=[greg, mreg], writes=[("merged", dc)])

            S.dma("sp", "lnbc", lnbc[:, 0:4096], lnv[l, 0:2].rearrange("a n -> (a n)").partition_broadcast(128), writes=["lnbc"])
            MREG = [("merged", c) for c in range(16)]
            for p in range(8):
                s, Wv = load_panel(w_out[l, p], 16, 256)
                for (t, R) in tiles_m:
                    b = bank()
                    for k in range(16):
                        S.op("pe", MM(ps[b][0:R, 0:256], merged[:, k, t * 128:t * 128 + R], Wv[:, k, :], k == 0, k == 15),
                             reads=[("w", s)] + MREG, writes=[("ps", b)])
                    xs = x[0:R, t, p * 256:(p + 1) * 256]
                    S.op("dve", STT(xs, xs, ALPHA, ps[b][0:R, 0:256], ALU.mult, ALU.add),
                         reads=[("ps", b), ("x", t)], writes=[("x", t)])
            ln_multi([(lambda o, n, t=t, R=R: x[0:R, t, o:o + n], R, ("x", t)) for (t, R) in tiles_m], D, 0, D)
            build_xT(tiles_m)
            for (t, R) in tiles_m:
                S.op("act", lambda e, t=t, R=R: e.mul(out=x[0:R, t, :], in_=x[0:R, t, :], mul=ALPHA),
                     reads=[("x", t)], writes=[("x", t)])

            S.dma("sp", "lnbc", lnbc[:, 0:4096], lnv[l, 2:4].rearrange("a n -> (a n)").partition_broadcast(128), writes=["lnbc"])
            fence(["meanb", "rstdb", "ptA", "ptB"], R_HT)
            def ffn_down(fb):
                hTb = hT[fb % 2]
                hreg = [("hT", fb % 2, 0), ("hT", fb % 2, 1)]
                s3, Wd = load_panel(w_down[l, fb], 2, 2048)
                for (t, R) in tiles_m:
                    for db in range(4):
                        b = bank()
                        for kc in range(2):
                            S.op("pe", MM(ps[b][0:R, 0:512], hTb[:, kc, t * 128:t * 128 + R], Wd[:, kc, db * 512:(db + 1) * 512],
                                          kc == 0, kc == 1),
                                 reads=[("w", s3)] + hreg, writes=[("ps", b)])
                        xs = x[0:R, t, db * 512:(db + 1) * 512]
                        S.op("dve", TT(xs, xs, ps[b][0:R, 0:512], ALU.add), reads=[("ps", b), ("x", t)], writes=[("x", t)])

            for fb in range(22):
                hTb = hT[fb % 2]
                hreg = [("hT", fb % 2, 0), ("hT", fb % 2, 1)]
                s1, Wgt = load_panel(w_up[l, fb], 16, 256)
                s2, Wup = load_panel(w_up[l, 22 + fb], 16, 256)
                for cc in range(2):
                    for (c0, n) in blocks_m:
                        bg = mm_fm(Wgt, s1, 16, cc, xT, TREG, c0, n)
                        gsb = gsig[gs_ctr % 3]
                        greg = ("gsig", gs_ctr % 3)
                        gs_ctr += 1
                        S.op("act", ACT(gsb[:, 0:n], ps[bg][:, 0:n], AF.Silu), reads=[("ps", bg)], writes=[greg])
                        bu = mm_fm(Wup, s2, 16, cc, xT, TREG, c0, n)
                        S.op("dve", TT(hTb[:, cc, c0:c0 + n], ps[bu][:, 0:n], gsb[:, 0:n], ALU.mult),
                             reads=[("ps", bu), greg], writes=[hreg[cc]])
                if fb >= 1:
                    ffn_down(fb - 1)
            ffn_down(21)
            ln_multi([(lambda o, n, t=t, R=R: x[0:R, t, o:o + n], R, ("x", t)) for (t, R) in tiles_m], D, 0, D)

        def WB3(pv_, l, wcol):
            base = pv_[:, l, wcol:wcol + 30]
            return bass.AP(base.tensor, base.offset, [list(base.ap[0]), [0, 16], [1, 30]])

        for gi in range(2):
            g = groups[gi]
            for (t, R) in g["tiles"]:
                r0 = g["row0"] + t * 128
                S.dma("sp", "xin", x[0:R, t, :], xin[r0:r0 + R, :], writes=[("x", t)])
            for l in range(DEPTH):
                load_layer_consts(l)
                build_xT(g["tiles"])
                layer(gi, l)
            if gi == 0:
                for t in range(1, 5):
                    S.dma("sp", "o_y", yp[(t - 1) * 128:t * 128, :], x[:, t, :], reads=[("x", t)])
            else:
                for t in range(4):
                    S.dma("sp", "o_y", yp[512 + t * 128:512 + (t + 1) * 128, :], x[:, t, :], reads=[("x", t)])
                S.dma("sp", "o_y", ys, x[0:16, 4, :], reads=[("x", 4)])

        cnt = S.emit(final_waits=["o_y", "o_vs", "o_ps", "o_pp", "o_cs", "o_cp"])
    return nc


_NC = None


def _prep_inputs(inp):
    f = lambda a: np.ascontiguousarray(np.asarray(a, dtype=np.float32))
    xp = f(inp["x_prompt"])
    xs = f(inp["x_sample"])
    sp = f(inp["state_pool"])
    sc = f(inp["state_conv"])
    b_in = f(inp["b_in"])
    L = DEPTH
    pvec = np.zeros((L, 128, PV_N), np.float32)
    fm = lambda v: v.reshape(L, -1, 128).transpose(0, 2, 1)
    pvec[:, :, PV_BIN:PV_BIN + 88] = fm(b_in)
    pvec[:, :, PV_BSC:PV_BSC + 8] = fm(f(inp["b_scale"]))
    pvec[:, :, PV_CBD:PV_CBD + 8] = fm(f(inp["c_b_dw"]))
    pvec[:, :, PV_CLG:PV_CLG + 8] = fm(f(inp["c_ln_g"]))
    pvec[:, :, PV_CLB:PV_CLB + 8] = fm(f(inp["c_ln_b"]))
    wdw = f(inp["c_w_dw"])
    pvec[:, :, PV_WDW:PV_WDW + 248] = wdw.reshape(L, 31, 8, 128).transpose(0, 3, 2, 1).reshape(L, 128, 248)
    rowv = np.stack([f(inp["a_ln_g"]), f(inp["a_ln_b"]), b_in[:, 1024:2048]], axis=1)
    lnv = np.stack([f(inp["ln1_g"]), f(inp["ln1_b"]), f(inp["ln2_g"]), f(inp["ln2_b"])], axis=1)
    a_ws = f(inp["a_ws"])
    awsT = np.ascontiguousarray(a_ws.transpose(0, 3, 1, 2)).reshape(L, 128, 1024)
    a_bs = f(inp["a_bs"])
    absd = a_bs.reshape(L, 1024)
    aw00 = np.ascontiguousarray(a_ws[:, :, 0, 0]).reshape(L, 8)
    abs0 = np.ascontiguousarray(a_bs[:, :, 0]).reshape(L, 8)
    ident = np.eye(128, dtype=np.float32)
    causT = np.triu(np.ones((128, 128), np.float32))
    def tile_kn(w, K, NP):
        return np.ascontiguousarray(f(w).reshape(L, K, 128, NP, 256).transpose(0, 3, 2, 1, 4)).reshape(L, NP, 128, K * 256)

    w_down_t = np.ascontiguousarray(f(inp["w_ffn_down"]).reshape(L, 22, 2, 128, D).transpose(0, 1, 3, 2, 4)).reshape(L, 22, 128, 2 * D)
    b_wg_t = np.ascontiguousarray(f(inp["b_w_group"]).reshape(L, 4, 2, 128, 256).transpose(0, 3, 1, 2, 4)).reshape(L, 128, 2048)
    shared = {
        "ident": ident, "causT": causT,
        "w_in": tile_kn(inp["w_in"], 16, 44), "w_a_out": tile_kn(inp["w_a_out"], 8, 8),
        "w_b_out": tile_kn(inp["w_b_out"], 8, 8), "w_c_out": tile_kn(inp["w_c_out"], 8, 8),
        "w_out": tile_kn(inp["w_out"], 16, 8), "w_ffn_up": tile_kn(inp["w_ffn_up"], 16, 44), "w_ffn_down": w_down_t,
        "b_w_group": b_wg_t, "pvec": pvec, "rowv": np.ascontiguousarray(rowv), "lnv": np.ascontiguousarray(lnv),
        "awsT": awsT, "a_bs": np.ascontiguousarray(absd), "a_w00": aw00, "a_bs0": abs0,
    }
    in_maps = []
    for c in range(NCORES):
        seq, half = c // 2, c % 2
        xin = np.zeros((NTOK, D), np.float32)
        if half == 1:
            xin[0:128] = xp[seq, 896:1024]
        xin[128:1152] = xp[seq, half * 1024:(half + 1) * 1024]
        xin[1152:1168] = xs[c * 16:(c + 1) * 16, 0, :]
        mask = np.full((128, 1), float(half), np.float32)
        invc = np.zeros((128, 64), np.float32)
        for gq, w in enumerate(POOL_W):
            for tt in range(16):
                cntv = float(w) if half == 1 else float(min(tt + 1, w))
                invc[:, gq * 16 + tt] = 1.0 / cntv
        m = dict(shared)
        m.update({"xin": xin, "mask": mask, "invc": invc,
                  "st_pool": np.ascontiguousarray(sp[:, c * 16:(c + 1) * 16]),
                  "st_conv": np.ascontiguousarray(sc[:, c * 16:(c + 1) * 16])})
        in_maps.append(m)
    return in_maps


def kernel(**inputs):
    global _NC
    if _NC is None:
        _NC = build_program()
    in_maps = _prep_inputs(inputs)
    res = run_bass_kernel_spmd(_NC, in_maps, core_ids=list(range(NCORES)))
    R = res.results
    y_prompt = np.zeros((4, 2048, D), np.float32)
    y_sample = np.zeros((128, 1, D), np.float32)
    pool_p = np.zeros((DEPTH, 4, 15, 1024), np.float32)
    conv_p = np.zeros((DEPTH, 4, 30, 1024), np.float32)
    pool_s = np.zeros((DEPTH, 128, 15, 1024), np.float32)
    conv_s = np.zeros((DEPTH, 128, 30, 1024), np.float32)
    v_s = np.zeros((DEPTH, 128, 1, 1024), np.float32)
    for c in range(NCORES):
        seq, half = c // 2, c % 2
        r = R[c]
        y_prompt[seq, half * 1024:(half + 1) * 1024] = r["yp"]
        y_sample[c * 16:(c + 1) * 16, 0] = r["ys"]
        if half == 1:
            pool_p[:, seq] = r["o_pool_p"]
            conv_p[:, seq] = r["o_conv_p"]
        pool_s[:, c * 16:(c + 1) * 16] = r["o_pool_s"]
        conv_s[:, c * 16:(c + 1) * 16] = r["o_conv_s"]
        v_s[:, c * 16:(c + 1) * 16, 0] = r["o_v_s"]
    return (y_prompt, y_sample, pool_p, conv_p, pool_s, conv_s, v_s)
```

```python
import numpy as np
from contextlib import ExitStack
import concourse.bass as bass
import concourse.mybir as mybir
from concourse.bass_utils import run_bass_kernel_spmd

F32 = mybir.dt.float32
BF16 = mybir.dt.bfloat16
AF = mybir.ActivationFunctionType
ALU = mybir.AluOpType
AX = mybir.AxisListType

D = 2048
NIN = 11264
DFF = 5632
DEPTH = 2
ALPHA = (2.0 * DEPTH) ** 0.25
EPS = 1e-5
NCORES = 8
TA = 640
TB = 528
NTOK = TA + TB
POOL_W = (2, 4, 8, 16)
POOL_CONV_CH = (2, 5, 7)

ENGS = ("pe", "act", "dve", "pool", "sp")


class Op:
    __slots__ = ("eng", "fn", "deps", "need_inc", "cnt", "dma", "sem", "val")

    def __init__(self, eng, fn, dma=False):
        self.eng = eng
        self.fn = fn
        self.deps = []
        self.need_inc = False
        self.cnt = None
        self.dma = dma
        self.sem = None
        self.val = None


class Sched:
    def __init__(self, nc, stack):
        self.nc = nc
        self.stack = stack
        self.ops = {e: [] for e in ENGS}
        self.all_ops = []
        self.last_w = {}
        self.readers = {}
        self.eng_sem = {e: stack.enter_context(nc.semaphore("s_" + e)) for e in ENGS}
        self.dma_sems = {}
        self.dma_val = {}
        self.dma_closed = {}

    def _track(self, op, reads, writes):
        deps = set()
        for r in reads:
            w = self.last_w.get(r)
            if w is not None:
                deps.add(w)
        for w_ in writes:
            w = self.last_w.get(w_)
            if w is not None:
                deps.add(w)
            for rd in self.readers.get(w_, {}).values():
                deps.add(rd)
        deps.discard(op)
        op.deps = [("d", d.sem, self.dma_val[d.sem]) if d.dma else d for d in deps]
        for d in deps:
            if d.dma:
                self.dma_closed[d.sem] = self.dma_val[d.sem]
        rkey = ("d", op.sem) if op.dma else op.eng
        for r in reads:
            self.readers.setdefault(r, {})[rkey] = op
        for w_ in writes:
            self.last_w[w_] = op
            self.readers[w_] = {}

    def op(self, eng, fn, reads=(), writes=()):
        o = Op(eng, fn)
        self._track(o, reads, writes)
        self.ops[eng].append(o)
        self.all_ops.append(o)
        return o

    def dma(self, queue, sem, out, in_, reads=(), writes=(), **kw):
        if sem not in self.dma_sems:
            self.dma_sems[sem] = self.stack.enter_context(self.nc.semaphore("d_" + sem))
            self.dma_val[sem] = 0

        def fn(e, out=out, in_=in_, kw=kw):
            return e.dma_start(out=out, in_=in_, **kw)

        o = Op(queue, fn, dma=True)
        o.sem = sem
        self._track(o, reads, writes)
        if self.dma_closed.get(sem):
            o.deps.append(("d", sem, self.dma_closed[sem]))
            self.dma_closed[sem] = None
        self.dma_val[sem] += 16
        o.val = self.dma_val[sem]
        self.ops[queue].append(o)
        self.all_ops.append(o)
        return o

    def emit(self, final_waits):
        nc = self.nc
        for o in self.all_ops:
            for d in o.deps:
                if not isinstance(d, tuple):
                    if d.eng == "pe" and o.eng == "pe":
                        continue
                    d.need_inc = True
        cnt = {e: 0 for e in ENGS}
        for e in ENGS:
            for o in self.ops[e]:
                if not o.dma and o.need_inc:
                    cnt[e] += 1
                    o.cnt = cnt[e]
        sched = self

        def run(ename, e):
            waited = {}
            for o in sched.ops[ename]:
                need = {}
                for d in o.deps:
                    if isinstance(d, tuple):
                        key = ("d", d[1])
                        v = d[2]
                    else:
                        if d.eng == "pe" and ename == "pe":
                            continue
                        key = ("e", d.eng)
                        v = d.cnt
                    if v > need.get(key, 0):
                        need[key] = v
                for key, v in need.items():
                    if waited.get(key, 0) >= v:
                        continue
                    waited[key] = v
                    s = sched.dma_sems[key[1]] if key[0] == "d" else sched.eng_sem[key[1]]
                    e.wait_ge(s, v)
                ins = o.fn(e)
                if o.dma:
                    ins.then_inc(sched.dma_sems[o.sem], 16)
                elif o.need_inc:
                    ins.then_inc(sched.eng_sem[ename], 1)
            if ename == "sp":
                for sem in final_waits:
                    e.wait_ge(sched.dma_sems[sem], sched.dma_val[sem])

        with nc.Block() as block:
            @block.tensor
            def _(e):
                run("pe", e)

            @block.scalar
            def _(e):
                run("act", e)

            @block.vector
            def _(e):
                run("dve", e)

            @block.gpsimd
            def _(e):
                run("pool", e)

            @block.sync
            def _(e):
                run("sp", e)
        return cnt


def MM(out, lhsT, rhs, start, stop):
    return lambda e: e.matmul(out, lhsT=lhsT, rhs=rhs, start=start, stop=stop)


def TR(out, in_, ident):
    return lambda e: e.transpose(out, in_, ident)


def ACT(out, in_, func, bias=None, scale=None):
    kw = {}
    if bias is not None:
        kw["bias"] = bias
    if scale is not None:
        kw["scale"] = scale
    return lambda e: e.activation(out=out, in_=in_, func=func, **kw)


def ACP(out, in_):
    return lambda e: e.copy(out=out, in_=in_)


def TT(out, in0, in1, op):
    return lambda e: e.tensor_tensor(out=out, in0=in0, in1=in1, op=op)


def TS(out, in0, s1, s2, op0, op1=None):
    if op1 is None:
        return lambda e: e.tensor_scalar(out=out, in0=in0, scalar1=s1, scalar2=None, op0=op0)
    return lambda e: e.tensor_scalar(out=out, in0=in0, scalar1=s1, scalar2=s2, op0=op0, op1=op1)


def STT(out, in0, scalar, in1, op0, op1):
    return lambda e: e.scalar_tensor_tensor(out=out, in0=in0, scalar=scalar, in1=in1, op0=op0, op1=op1)


def CP(out, in_):
    return lambda e: e.tensor_copy(out=out, in_=in_)


def MSET(ap, v):
    return lambda e: e.memset(ap, v)


PV_BIN = 0
PV_BSC = 88
PV_CBD = 96
PV_CLG = 104
PV_CLB = 112
PV_WDW = 120
PV_N = 368


def build_program():
    nc = bass.Bass("TRN2", target_bir_lowering=False)

    def din(name, shape):
        return nc.dram_tensor(name, list(shape), F32, kind="ExternalInput").ap()

    def dout(name, shape):
        return nc.dram_tensor(name, list(shape), F32, kind="ExternalOutput").ap()

    xin = din("xin", [NTOK, D])
    maskd = din("mask", [128, 1])
    invcd = din("invc", [128, 64])
    identd = din("ident", [128, 128])
    causd = din("causT", [128, 128])
    st_pool = din("st_pool", [DEPTH, 16, 15, 1024])
    st_conv = din("st_conv", [DEPTH, 16, 30, 1024])
    w_in = din("w_in", [DEPTH, 44, 128, 4096])
    w_a_out = din("w_a_out", [DEPTH, 8, 128, 2048])
    w_b_out = din("w_b_out", [DEPTH, 8, 128, 2048])
    w_c_out = din("w_c_out", [DEPTH, 8, 128, 2048])
    w_out = din("w_out", [DEPTH, 8, 128, 4096])
    w_up = din("w_ffn_up", [DEPTH, 44, 128, 4096])
    w_down = din("w_ffn_down", [DEPTH, 22, 128, 4096])
    b_wg = din("b_w_group", [DEPTH, 128, 2048])
    pvec = din("pvec", [DEPTH, 128, PV_N])
    rowv = din("rowv", [DEPTH, 3, 1024])
    lnv = din("lnv", [DEPTH, 4, D])
    awsT = din("awsT", [DEPTH, 128, 8 * 128])
    absd = din("a_bs", [DEPTH, 8 * 128])
    aw00 = din("a_w00", [DEPTH, 8])
    abs0 = din("a_bs0", [DEPTH, 8])

    yp = dout("yp", [1024, D])
    ys = dout("ys", [16, D])
    o_pool_p = dout("o_pool_p", [DEPTH, 15, 1024])
    o_conv_p = dout("o_conv_p", [DEPTH, 30, 1024])
    o_pool_s = dout("o_pool_s", [DEPTH, 16, 15, 1024])
    o_conv_s = dout("o_conv_s", [DEPTH, 16, 30, 1024])
    o_v_s = dout("o_v_s", [DEPTH, 16, 1024])

    with ExitStack() as st:
        S = Sched(nc, st)

        def sb(name, shape, dt=F32):
            return st.enter_context(nc.sbuf_tensor(name, list(shape), dt))

        x = sb("x", [128, 5, D])
        xT = sb("xT", [128, 16, TA], BF16)
        uT = sb("uT", [128, 8, TA], BF16)
        bmix = sb("bmix", [128, 8, TA], BF16)
        cact = sb("cact", [128, 8, TA], BF16)
        arena = sb("arena", [128, 10480])
        scr = sb("scr", [128, 1310])
        fscr = sb("fscr", [128, 2])
        epsT = sb("epsT", [128, 1])
        lnbc = sb("lnbc", [128, 4096])
        wsl = [sb(f"wsl{i}", [128, 4096], BF16) for i in range(4)]
        pv = sb("pv", [128, DEPTH, PV_N])
        ident = sb("ident_s", [128, 128])
        identb = sb("identb", [128, 128], BF16)
        ones32 = sb("ones32", [128, 128])
        wmT = sb("wmT", [128, 8, 128], BF16)
        wdg = sb("wdg", [128, 8, 128], BF16)
        w00 = sb("w00", [128, 8])
        bs0 = sb("bs0", [128, 8])
        mask = sb("mask_s", [128, 1])
        invc = sb("invc_s", [128, 64])
        pprefix = sb("pprefix", [128, DEPTH, 8, 15])
        cprefix = sb("cprefix", [128, DEPTH, 8, 30])
        stats = [sb(f"stats{i}", [128, 64]) for i in range(5)]
        tmpA = [sb(f"tmpA{i}", [128, 512]) for i in range(2)]
        sq = [sb(f"sq{i}", [128, 320]) for i in range(2)]
        ptA = scr[:, 0:655]
        ptB = scr[:, 655:1310]
        meanb = scr[:, 0:640]
        rstdb = scr[:, 640:1280]
        hT = [scr.bitcast(BF16)[:, i * 1280:(i + 1) * 1280].rearrange("p (a b) -> p a b", b=TA) for i in range(2)]
        gsig = [sb(f"gsig{i}", [128, 320]) for i in range(3)]
        awf = lnbc[:, 0:1024]
        caus = lnbc[:, 1024:1152]
        bsbc = lnbc[:, 3072:4096].rearrange("p (h i) -> p h i", i=128)

        def RG(name, n):
            return [(name, i) for i in range(n)]

        R_XB = RG("xb16", 2)
        R_HT = [("hT", i, c) for i in range(2) for c in range(2)]

        def fence(old, new):
            S.op("dve", MSET(fscr[:, 0:1], 0.0), writes=list(old) + list(new))
        ps = [st.enter_context(nc.psum_tensor(f"ps{i}", [128, 512], F32)) for i in range(8)]
        psb = [p.bitcast(BF16) for p in ps]

        def aview(off, shape, dt=F32):
            n = 1
            for s_ in shape[1:]:
                n *= s_
            if dt == F32:
                ap = arena[:, off:off + n]
            else:
                ap = arena.bitcast(BF16)[:, 2 * off:2 * off + n]
            if len(shape) == 3:
                ap = ap.rearrange("p (a b) -> p a b", b=shape[2])
            return ap

        vraw = aview(0, [128, 5, 1024])
        vbf = aview(5120, [128, 5, 1024], BF16)
        xb16 = [aview(8000 + i * 1024, [128, 2048], BF16) for i in range(2)]
        hbx = aview(0, [128, 8, 15 + TA])
        pooled = aview(5240, [128, 8, TA], BF16)
        glxb = aview(0, [128, 8, 30 + TA], BF16)
        gtail = aview(2680, [128, 8, 46])
        dgb = aview(3048, [128, 31, 128], BF16)
        conv = aview(5360, [128, 8, TA])
        merged = aview(0, [128, 16, TA], BF16)
        macc = aview(5120, [128, 2, TA])
        groups = [
            dict(T=TA, tiles=[(0, 128), (1, 128), (2, 128), (3, 128), (4, 128)], blocks=[(0, 320), (320, 320)],
                 row0=0, sample=False),
            dict(T=TB, tiles=[(0, 128), (1, 128), (2, 128), (3, 128), (4, 16)], blocks=[(0, 264), (264, 264)],
                 row0=TA, sample=True),
        ]

        bank_ctr = [0]

        def bank():
            b = bank_ctr[0] % 8
            bank_ctr[0] += 1
            return b

        slot_ctr = [0]

        def load_panel(src_ap, K, C):
            s = slot_ctr[0] % 4
            slot_ctr[0] += 1
            S.dma("pool", f"w{s}", wsl[s][:, 0:K * C], src_ap, writes=[("w", s)])
            return s, wsl[s][:, 0:K * C].rearrange("p (k c) -> p k c", c=C)

        XT_ALL = [("xT", t) for t in range(5)]

        S.dma("sp", "cst", ident[:], identd, writes=["ident"])
        S.dma("sp", "cst", mask[:], maskd, writes=["mask"])
        S.dma("sp", "cst", invc[:], invcd, writes=["invc"])
        S.dma("sp", "cst", pv[:], pvec.rearrange("l p n -> p l n"), writes=["pv"])
        S.op("dve", CP(identb[:], ident[:]), reads=["ident"], writes=["identb"])
        S.op("dve", MSET(ones32[:], 1.0), writes=["ones32"])
        S.op("dve", MSET(epsT[:], EPS), writes=["epsT"])

        def PV(l, col, n=1):
            return pv[:, l, col:col + n]

        def load_layer_consts(l):
            S.dma("sp", "lnbc", awf, awsT[l], writes=["lnbc"])
            S.dma("sp", "lnbc", caus, causd, writes=[])
            S.dma("sp", "cst", w00[:], aw00[l].partition_broadcast(128), writes=["w00"])
            S.dma("sp", "cst", bs0[:], abs0[l].partition_broadcast(128), writes=["bs0"])
            for h in range(8):
                S.op("dve", TT(wmT[:, h, :], awf[:, h * 128:(h + 1) * 128], caus, ALU.mult),
                     reads=["lnbc"], writes=["wmT"])
                S.op("dve", TS(wdg[:, h, :], ident[:], w00[:, h:h + 1], None, ALU.mult),
                     reads=["ident", "w00"], writes=["wdg"])

        def build_xT(tl):
            for i, (t, R) in enumerate(tl):
                xb = xb16[i % 2]
                S.op("act", ACP(xb[0:R, :], x[0:R, t, :]), reads=[("x", t)], writes=[("xb16", i % 2)])
                for half in range(2):
                    b = bank()
                    for kk in range(8):
                        k = half * 8 + kk
                        S.op("pe", TR(psb[b][:, kk * 128:kk * 128 + R], xb[0:R, k * 128:(k + 1) * 128], identb[0:R, 0:R]),
                             reads=[("xb16", i % 2), "identb"], writes=[("ps", b)])
                    src = psb[b][:, 0:1024].rearrange("p (k c) -> p k c", c=128)[:, :, 0:R]
                    S.op("dve", CP(xT[:, half * 8:half * 8 + 8, t * 128:t * 128 + R], src),
                         reads=[("ps", b)], writes=[("xT", t)])

        def mm_fm(Wv, s, K, cc, src, src_regs, c0, n):
            if src_regs and src_regs[0][0] == "xT":
                src_regs = [("xT", t) for t in range(c0 // 128, (c0 + n - 1) // 128 + 1)]
            b = bank()
            for k in range(K):
                S.op("pe", MM(ps[b][:, 0:n], Wv[:, k, cc * 128:(cc + 1) * 128], src[:, k, c0:c0 + n], k == 0, k == K - 1),
                     reads=[("w", s)] + src_regs, writes=[("ps", b)])
            return b

        def ln_multi(items, Dn, gcol, bcol, pre=0):
            for _ in ln_stages(items, Dn, gcol, bcol, pre):
                pass

        def bn_piece(t, R, c, ap, reg):
            S.op("dve", lambda e: e.bn_stats(out=stats[t][0:R, c * 6:(c + 1) * 6], in_=ap), reads=[reg], writes=[("st", t)])

        def ln_stages(items, Dn, gcol, bcol, pre=0):
            nch = pre if pre else Dn // 512
            its = []
            for (ap_fn, R, reg, ti) in items:
                its.append((ap_fn, R, reg, stats[ti], ("st", ti)))
            for (ap_fn, R, reg, stt, sreg) in its:
                if not pre:
                    for c in range(nch):
                        S.op("dve", lambda e, c=c, stt=stt, R=R, ap_fn=ap_fn: e.bn_stats(out=stt[0:R, c * 6:(c + 1) * 6], in_=ap_fn(c * 512, 512)),
                             reads=[reg], writes=[sreg])
                S.op("dve", lambda e, stt=stt, R=R: e.bn_aggr(out=stt[0:R, 48:50], in_=stt[0:R, 0:nch * 6]), reads=[sreg], writes=[sreg])
            yield 0
            for (ap_fn, R, reg, stt, sreg) in its:
                S.op("act", ACT(stt[0:R, 50:51], stt[0:R, 49:50], AF.Sqrt, bias=epsT[0:R, 0:1], scale=1.0),
                     reads=[sreg, "epsT"], writes=[sreg])
            yield 1
            for (ap_fn, R, reg, stt, sreg) in its:
                S.op("dve", lambda e, stt=stt, R=R: e.reciprocal(out=stt[0:R, 50:51], in_=stt[0:R, 50:51]), reads=[sreg], writes=[sreg])
                S.op("dve", STT(stt[0:R, 51:52], stt[0:R, 48:49], -1.0, stt[0:R, 50:51], ALU.mult, ALU.mult),
                     reads=[sreg], writes=[sreg])
            yield 2
            for (ap_fn, R, reg, stt, sreg) in its:
                full = ap_fn(0, Dn)
                S.op("act", ACT(full, full, AF.Identity, bias=stt[0:R, 51:52], scale=stt[0:R, 50:51]),
                     reads=[reg, sreg], writes=[reg])
            yield 3
            for (ap_fn, R, reg, stt, sreg) in its:
                full = ap_fn(0, Dn)
                S.op("dve", TT(full, full, lnbc[0:R, gcol:gcol + Dn], ALU.mult), reads=[reg, "lnbc"], writes=[reg])
            yield 4
            for (ap_fn, R, reg, stt, sreg) in its:
                full = ap_fn(0, Dn)
                S.op("dve", TT(full, full, lnbc[0:R, bcol:bcol + Dn], ALU.add), reads=[reg, "lnbc"], writes=[reg])
            yield 5

        def layer(gi, l):
            g = groups[gi]
            T = g["T"]
            tiles = g["tiles"]
            blocks = g["blocks"]
            sample = g["sample"]
            TREG = [("xT", t) for t, _ in tiles]
            light = (gi == 0 and l == DEPTH - 1)
            tiles_m = tiles[1:] if light else tiles
            if light:
                blocks_m, Tm0 = [(128, 256), (384, 256)], 128
            elif gi == 0:
                blocks_m, Tm0 = [(96, 272), (368, 272)], 96
            else:
                blocks_m, Tm0 = blocks, 0
            blocks_w = [(64, 288), (352, 288)] if gi == 0 else blocks

            def au_panel(p):
                s, Wv = load_panel(w_in[l, p], 16, 256)
                for cc in range(2):
                    ch = p * 2 + cc
                    for (c0, n) in blocks_m:
                        b = mm_fm(Wv, s, 16, cc, xT, TREG, c0, n)
                        S.op("act", ACT(uT[:, ch, c0:c0 + n], ps[b][:, 0:n], AF.Gelu, bias=PV(l, PV_BIN + ch), scale=1.0),
                             reads=[("ps", b), "pv"], writes=[("uT", ch)])

            fence(RG("merged", 16) + RG("macc", 2) + R_XB, RG("vraw", 5) + RG("vbf", 5))
            S.dma("sp", "lnbc", lnbc[:, 0:3072], rowv[l].rearrange("a n -> (a n)").partition_broadcast(128), writes=["lnbc"])
            S.dma("sp", "lnbc", lnbc[:, 3072:4096], absd[l].partition_broadcast(128), writes=[])
            for p in range(4):
                s, Wv = load_panel(w_in[l, 4 + p], 16, 256)
                for (t, R) in tiles_m:
                    b = bank()
                    for k in range(16):
                        S.op("pe", MM(ps[b][0:R, 0:256], xT[:, k, t * 128:t * 128 + R], Wv[:, k, :], k == 0, k == 15),
                             reads=[("w", s), ("xT", t)], writes=[("ps", b)])
                    vs_ = vraw[0:R, t, p * 256:(p + 1) * 256]
                    S.op("dve", TT(vs_, ps[b][0:R, 0:256], lnbc[0:R, 2048 + p * 256:2048 + (p + 1) * 256], ALU.add),
                         reads=[("ps", b), "lnbc"], writes=[("vraw", t)])
                    S.op("act", ACT(vs_, vs_, AF.Gelu), reads=[("vraw", t)], writes=[("vraw", t)])
                    bn_piece(t, R, p, vs_, ("vraw", t))
            au_next = 0
            for stg in ln_stages([(lambda o, n, t=t, R=R: vraw[0:R, t, o:o + n], R, ("vraw", t), t) for (t, R) in tiles_m],
                                 1024, 0, 1024, pre=4):
                if stg in (0, 2, 3, 5):
                    au_panel(au_next)
                    au_next += 1
            for (t, R) in tiles_m:
                S.op("act", ACP(vbf[0:R, t, :], vraw[0:R, t, :]), reads=[("vraw", t)], writes=[("vbf", t)])
            if sample:
                S.dma("sp", "o_vs", o_v_s[l], vraw[0:16, 4, :], reads=[("vraw", 4)])
            for (t, R) in tiles_m:
                is_s = sample and t == 4
                for hh in range(2):
                    b = bank()
                    for h4 in range(4):
                        h = hh * 4 + h4
                        rhs = wdg[0:R, h, 0:R] if is_s else wmT[:, h, :]
                        S.op("pe", MM(ps[b][:, h4 * 128:h4 * 128 + R], vbf[0:R, t, h * 128:(h + 1) * 128], rhs, True, True),
                             reads=[("vbf", t), "wdg" if is_s else "wmT"], writes=[("ps", b)])
                    tm = tmpA[hh]
                    pv3 = ps[b][:, 0:512].rearrange("p (h c) -> p h c", c=128)[:, :, 0:R]
                    tm3 = tm[:, 0:512].rearrange("p (h c) -> p h c", c=128)[:, :, 0:R]
                    bsrc = bsbc[:, hh * 4:hh * 4 + 4, 0:R]
                    if is_s:
                        for h4 in range(4):
                            h = hh * 4 + h4
                            S.op("dve", TS(tm[:, h4 * 128:h4 * 128 + R], ps[b][:, h4 * 128:h4 * 128 + R], bs0[:, h:h + 1], None, ALU.add),
                                 reads=[("ps", b), "bs0"], writes=[("tmpA", hh)])
                    else:
                        S.op("dve", TT(tm3, pv3, bsrc, ALU.add), reads=[("ps", b), "lnbc"], writes=[("tmpA", hh)])
                    dst = uT[:, hh * 4:hh * 4 + 4, t * 128:t * 128 + R]
                    S.op("dve", TT(dst, tm3, dst, ALU.mult), reads=[("tmpA", hh)] + [("uT", hh * 4 + i) for i in range(4)],
                         writes=[("uT", hh * 4 + i) for i in range(4)])

            fence(RG("vraw", 5) + RG("vbf", 5) + R_HT, RG("hbx", 8) + RG("pooled", 8) + ["ptA", "ptB"])
            if gi == 0:
                S.op("dve", MSET(hbx[:, :, 0:15], 0.0), reads=[], writes=[("hbx", c) for c in range(8)])
            else:
                S.op("dve", CP(hbx[:, :, 0:15], pprefix[:, l, :, :]), reads=[("pprefix", l)],
                     writes=[("hbx", c) for c in range(8)])
            if sample:
                S.dma("sp", "lnbc", lnbc[0:120, 0:2048].rearrange("p (a n) -> p a n", a=2),
                      st_pool[l].rearrange("(a b) k n -> (b k) a n", a=2), writes=["lnbc"])
                S.dma("sp", "o_ps", o_pool_s[l][:, 0:14, :], st_pool[l][:, 1:15, :])
            for p in range(4):
                s, Wv = load_panel(w_in[l, 8 + p], 16, 256)
                for cc in range(2):
                    ch = p * 2 + cc
                    for (c0, n) in blocks_w:
                        b = mm_fm(Wv, s, 16, cc, xT, TREG, c0, n)
                        S.op("act", ACT(hbx[:, ch, 15 + c0:15 + c0 + n], ps[b][:, 0:n], AF.Identity,
                                        bias=PV(l, PV_BIN + 16 + ch), scale=1.0),
                             reads=[("ps", b), "pv"], writes=[("hbx", ch)])
            sg = slot_ctr[0] % 4
            slot_ctr[0] += 1
            WG = wsl[sg][:, 0:2048].rearrange("p (g c d) -> p g c d", g=4, c=2)
            S.dma("pool", f"w{sg}", wsl[sg][:, 0:2048], b_wg[l], writes=[("w", sg)])
            for ch in range(8):
                hreg = ("hbx", ch)
                if gi == 0:
                    S.op("dve", TS(hbx[:, ch, 15:15 + 128], hbx[:, ch, 15:15 + 128], mask[:, 0:1], None, ALU.mult),
                         reads=[hreg, "mask"], writes=[hreg])
                gq = ch // 2
                w = POOL_W[gq]
                ext = hbx[:, ch, :]
                L = 15 + T
                cur = ext
                off = 0
                steps = [(1, ptA), (2, ptB), (4, ptA), (8, ptB)][:gq + 1]
                for (sh, dstbuf) in steps:
                    no = off + sh
                    S.op("dve", TT(dstbuf[:, no:L], cur[:, no:L], cur[:, off:L - sh], ALU.add),
                         reads=[hreg, "ptA", "ptB"], writes=["ptA" if dstbuf is ptA else "ptB"])
                    cur = dstbuf
                    off = no
                S.op("dve", STT(pooled[:, ch, 0:T], cur[:, 15:15 + T], 1.0 / w, ext[:, 15:15 + T], ALU.mult, ALU.subtract),
                     reads=[hreg, "ptA", "ptB"], writes=[("pooled", ch)])
                if gi == 0:
                    S.op("dve", TT(tmpA[0][:, 0:16], cur[:, 15 + 128:15 + 144], invc[:, gq * 16:gq * 16 + 16], ALU.mult),
                         reads=["ptA", "ptB", "invc"], writes=[("tmpA", 0)])
                    S.op("dve", TT(pooled[:, ch, 128:144], tmpA[0][:, 0:16], ext[:, 15 + 128:15 + 144], ALU.subtract),
                         reads=[("tmpA", 0), hreg], writes=[("pooled", ch)])
                    S.op("dve", CP(pprefix[:, l, ch, :], ext[:, T:T + 15]), reads=[hreg], writes=[("pprefix", l)])
                if sample:
                    b = bank()
                    for a in range(2):
                        S.op("pe", TR(ps[b][:, a * 120:(a + 1) * 120], lnbc[0:120, a * 1024 + ch * 128:a * 1024 + (ch + 1) * 128],
                                      ident[0:120, 0:120]), reads=["lnbc", "ident"], writes=[("ps", b)])
                    stv = ps[b][:, 0:240].rearrange("p (b k) -> p b k", k=15)
                    hnew = hbx[:, ch, 15 + 512:15 + 528]
                    if w > 2:
                        S.op("dve", lambda e, stv=stv, w=w: e.reduce_sum(out=tmpA[1][:, 0:16], in_=stv[:, :, 16 - w:15], axis=AX.X),
                             reads=[("ps", b)], writes=[("tmpA", 1)])
                    else:
                        S.op("dve", CP(tmpA[1][:, 0:16], stv[:, :, 14]), reads=[("ps", b)], writes=[("tmpA", 1)])
                    S.op("dve", TT(tmpA[1][:, 0:16], tmpA[1][:, 0:16], hnew, ALU.add), reads=[("tmpA", 1), hreg], writes=[("tmpA", 1)])
                    S.op("dve", STT(pooled[:, ch, 512:528], tmpA[1][:, 0:16], 1.0 / w, hnew, ALU.mult, ALU.subtract),
                         reads=[("tmpA", 1), hreg], writes=[("pooled", ch)])
            if sample:
                b = bank()
                b2 = bank()
                for ch in range(8):
                    bb = b if ch < 4 else b2
                    S.op("pe", TR(ps[bb][0:15, (ch % 4) * 128:(ch % 4 + 1) * 128], hbx[:, ch, 15 + 497:15 + 512], ident[:, :]),
                         reads=[("hbx", ch), "ident"], writes=[("ps", bb)])
                S.op("dve", CP(tmpA[0][0:15, 0:512], ps[b][0:15, 0:512]), reads=[("ps", b)], writes=[("tmpA", 0)])
                S.op("dve", CP(tmpA[1][0:15, 0:512], ps[b2][0:15, 0:512]), reads=[("ps", b2)], writes=[("tmpA", 1)])
                S.dma("sp", "o_pp", o_pool_p[l][:, 0:512], tmpA[0][0:15, 0:512], reads=[("tmpA", 0)])
                S.dma("sp", "o_pp", o_pool_p[l][:, 512:1024], tmpA[1][0:15, 0:512], reads=[("tmpA", 1)])
                b = bank()
                b2 = bank()
                for ch in range(8):
                    bb = b if ch < 4 else b2
                    S.op("pe", TR(ps[bb][0:16, (ch % 4) * 128:(ch % 4 + 1) * 128], hbx[:, ch, 15 + 512:15 + 528], ident[:, :]),
                         reads=[("hbx", ch), "ident"], writes=[("ps", bb)])
                S.op("dve", CP(tmpA[0][0:16, 0:512], ps[b][0:16, 0:512]), reads=[("ps", b)], writes=[("tmpA", 0)])
                S.op("dve", CP(tmpA[1][0:16, 0:512], ps[b2][0:16, 0:512]), reads=[("ps", b2)], writes=[("tmpA", 1)])
                S.dma("sp", "o_ps", o_pool_s[l][:, 14, 0:512], tmpA[0][0:16, 0:512], reads=[("tmpA", 0)])
                S.dma("sp", "o_ps", o_pool_s[l][:, 14, 512:1024], tmpA[1][0:16, 0:512], reads=[("tmpA", 1)])
            for gq in range(4):
                for dd in range(2):
                    ch = gq * 2 + dd
                    for (c0, n) in blocks_m:
                        b = bank()
                        for kc in range(2):
                            S.op("pe", MM(ps[b][:, 0:n], WG[:, gq, kc, dd * 128:(dd + 1) * 128], pooled[:, gq * 2 + kc, c0:c0 + n],
                                          kc == 0, kc == 1),
                                 reads=[("w", sg), ("pooled", gq * 2), ("pooled", gq * 2 + 1)], writes=[("ps", b)])
                        S.op("act", ACT(bmix[:, ch, c0:c0 + n], ps[b][:, 0:n], AF.Identity, scale=PV(l, PV_BSC + ch)),
                             reads=[("ps", b), "pv"], writes=[("bmix", ch)])

            fence(RG("hbx", 8) + RG("pooled", 8) + ["ptA", "ptB"] + R_XB,
                  RG("glxb", 8) + RG("gtail", 8) + ["dg"] + RG("conv", 8) + ["meanb", "rstdb"])
            if gi == 0:
                S.op("dve", MSET(glxb[:, :, 0:30], 0.0), reads=[], writes=RG("glxb", 8))
            else:
                S.op("dve", CP(glxb[:, :, 0:30], cprefix[:, l, :, :]), reads=[("cprefix", l)], writes=RG("glxb", 8))
            if sample:
                S.dma("sp", "lnbc", lnbc[0:120, 0:4096].rearrange("p (a n) -> p a n", a=4),
                      st_conv[l].rearrange("(a b) k n -> (b k) a n", a=4), writes=["lnbc"])
                S.dma("sp", "o_cs", o_conv_s[l][:, 0:29, :], st_conv[l][:, 1:30, :])
            ntail = 46 if sample else 30
            tail_lo = (482 - blocks_w[1][0]) if sample else (blocks_w[1][1] - 30)
            gs_ctr = 0
            idb_bc = bass.AP(identb[:, :].tensor, identb[:, :].offset, [list(identb[:, :].ap[0]), [0, 31], [1, 128]])

            def conv_pe(ch):
                wcol = PV_WDW + ch * 31
                for (c0, n) in blocks_m:
                    b = bank()
                    for k in range(31):
                        S.op("pe", MM(ps[b][:, 0:n], dgb[:, k, :], glxb[:, ch, c0 + k:c0 + k + n], k == 0, k == 30),
                             reads=["dg", ("glxb", ch)], writes=[("ps", b)])
                    S.op("act", ACT(conv[:, ch, c0:c0 + n], ps[b][:, 0:n], AF.Identity, bias=PV(l, PV_CBD + ch), scale=1.0),
                         reads=[("ps", b), "pv"], writes=[("conv", ch)])
                if sample:
                    gnew = gtail[:, ch, 30:46]
                    b = bank()
                    for a in range(4):
                        S.op("pe", TR(ps[b][:, a * 120:(a + 1) * 120],
                                      lnbc[0:120, a * 1024 + ch * 128:a * 1024 + (ch + 1) * 128], ident[0:120, 0:120]),
                             reads=["lnbc", "ident"], writes=[("ps", b)])
                    stv = ps[b][:, 0:480].rearrange("p (b k) -> p b k", k=30)
                    tm3 = tmpA[0][:, 0:480].rearrange("p (b k) -> p b k", k=30)
                    S.op("dve", TT(tm3, stv, WB3(pv, l, wcol), ALU.mult), reads=[("ps", b), "pv"], writes=[("tmpA", 0)])
                    S.op("dve", lambda e, tm3=tm3: e.reduce_sum(out=tmpA[1][:, 0:16], in_=tm3, axis=AX.X),
                         reads=[("tmpA", 0)], writes=[("tmpA", 1)])
                    S.op("dve", TS(tmpA[1][:, 16:32], gnew, PV(l, wcol + 30), PV(l, PV_CBD + ch), ALU.mult, ALU.add),
                         reads=[("gtail", ch), "pv"], writes=[("tmpA", 1)])
                    S.op("dve", TT(conv[:, ch, 512:528], tmpA[1][:, 0:16], tmpA[1][:, 16:32], ALU.add),
                         reads=[("tmpA", 1)], writes=[("conv", ch)])

            for ch in range(8):
                p, cc = ch // 2, ch % 2
                if cc == 0:
                    sg_, Wgt_ = load_panel(w_in[l, 16 + p], 16, 256)
                    sv_, Wvl_ = load_panel(w_in[l, 12 + p], 16, 256)
                for bi_, (c0, n) in enumerate(blocks_w):
                    bg = mm_fm(Wgt_, sg_, 16, cc, xT, TREG, c0, n)
                    gsb = gsig[gs_ctr % 3]
                    greg = ("gsig", gs_ctr % 3)
                    gs_ctr += 1
                    S.op("act", ACT(gsb[:, 0:n], ps[bg][:, 0:n], AF.Sigmoid, bias=PV(l, PV_BIN + 32 + ch), scale=1.0),
                         reads=[("ps", bg), "pv"], writes=[greg])
                    bv = mm_fm(Wvl_, sv_, 16, cc, xT, TREG, c0, n)
                    S.op("dve", STT(glxb[:, ch, 30 + c0:30 + c0 + n], ps[bv][:, 0:n], PV(l, PV_BIN + 24 + ch), gsb[:, 0:n],
                                    ALU.add, ALU.mult),
                         reads=[("ps", bv), greg, "pv"], writes=[("glxb", ch)])
                    if bi_ == 1:
                        S.op("dve", STT(gtail[:, ch, 0:ntail], ps[bv][:, tail_lo:tail_lo + ntail], PV(l, PV_BIN + 24 + ch),
                                        gsb[:, tail_lo:tail_lo + ntail], ALU.add, ALU.mult),
                             reads=[("ps", bv), greg, "pv"], writes=[("gtail", ch)])
                if gi == 0:
                    S.op("dve", TS(glxb[:, ch, 30:30 + 128], glxb[:, ch, 30:30 + 128], mask[:, 0:1], None, ALU.mult),
                         reads=[("glxb", ch), "mask"], writes=[("glxb", ch)])
                    S.op("dve", CP(cprefix[:, l, ch, :], gtail[:, ch, 0:30]), reads=[("gtail", ch)], writes=[("cprefix", l)])
                if ch >= 1:
                    conv_pe(ch - 1)
                wcol = PV_WDW + ch * 31
                wb_ = pv[:, l, wcol:wcol + 31]
                w_bc = bass.AP(wb_.tensor, wb_.offset, [list(wb_.ap[0]), [1, 31], [0, 128]])
                S.op("dve", TT(dgb, idb_bc, w_bc, ALU.mult), reads=["identb", "pv"], writes=["dg"])
            conv_pe(7)
            if sample:
                for (n_r, c_lo, dst_ap, semn) in ((30, 0, o_conv_p[l], "o_cp"), (16, 30, o_conv_s[l][:, 29, :], "o_cs")):
                    b = bank()
                    b2 = bank()
                    for ch in range(8):
                        bb = b if ch < 4 else b2
                        S.op("pe", TR(ps[bb][0:n_r, (ch % 4) * 128:(ch % 4 + 1) * 128], gtail[:, ch, c_lo:c_lo + n_r], ident[:, :]),
                             reads=[("gtail", ch), "ident"], writes=[("ps", bb)])
                    S.op("dve", CP(tmpA[0][0:n_r, 0:512], ps[b][0:n_r, 0:512]), reads=[("ps", b)], writes=[("tmpA", 0)])
                    S.op("dve", CP(tmpA[1][0:n_r, 0:512], ps[b2][0:n_r, 0:512]), reads=[("ps", b2)], writes=[("tmpA", 1)])
                    S.dma("sp", semn, dst_ap[:, 0:512], tmpA[0][0:n_r, 0:512], reads=[("tmpA", 0)])
                    S.dma("sp", semn, dst_ap[:, 512:1024], tmpA[1][0:n_r, 0:512], reads=[("tmpA", 1)])
            CONV_ALL = [("conv", c) for c in range(8)]
            sq_ctr = 0
            for (c0, n) in blocks_m:
                b1 = bank()
                for ch in range(8):
                    S.op("pe", MM(ps[b1][:, 0:n], ones32[:, :], conv[:, ch, c0:c0 + n], ch == 0, ch == 7),
                         reads=[("conv", ch), "ones32"], writes=[("ps", b1)])
                b2 = bank()
                for ch in range(8):
                    sqb = sq[sq_ctr % 2]
                    sreg = ("sq", sq_ctr % 2)
                    sq_ctr += 1
                    S.op("act", ACT(sqb[:, 0:n], conv[:, ch, c0:c0 + n], AF.Square), reads=[("conv", ch)], writes=[sreg])
                    S.op("pe", MM(ps[b2][:, 0:n], ones32[:, :], sqb[:, 0:n], ch == 0, ch == 7),
                         reads=[sreg, "ones32"], writes=[("ps", b2)])
                mb = meanb[:, c0:c0 + n]
                rb = rstdb[:, c0:c0 + n]
                S.op("dve", TS(mb, ps[b1][:, 0:n], 1.0 / 1024, None, ALU.mult), reads=[("ps", b1)], writes=["meanb"])
                S.op("dve", TT(tmpA[0][:, 0:n], mb, mb, ALU.mult), reads=["meanb"], writes=[("tmpA", 0)])
                S.op("dve", STT(rb, ps[b2][:, 0:n], 1.0 / 1024, tmpA[0][:, 0:n], ALU.mult, ALU.subtract),
                     reads=[("ps", b2), ("tmpA", 0)], writes=["rstdb"])
                S.op("dve", TS(rb, rb, 0.0, None, ALU.max), reads=["rstdb"], writes=["rstdb"])
                S.op("act", ACT(rb, rb, AF.Sqrt, bias=epsT[:, 0:1], scale=1.0), reads=["rstdb", "epsT"], writes=["rstdb"])
                S.op("dve", lambda e, rb=rb: e.reciprocal(out=rb, in_=rb), reads=["rstdb"], writes=["rstdb"])
            for ch in range(8):
                cv = conv[:, ch, Tm0:T]
                S.op("dve", TT(cv, cv, meanb[:, Tm0:T], ALU.subtract), reads=[("conv", ch), "meanb"], writes=[("conv", ch)])
                S.op("dve", TT(cv, cv, rstdb[:, Tm0:T], ALU.mult), reads=[("conv", ch), "rstdb"], writes=[("conv", ch)])
                S.op("act", ACT(cact[:, ch, Tm0:T], cv, AF.Silu, bias=PV(l, PV_CLB + ch), scale=PV(l, PV_CLG + ch)),
                     reads=[("conv", ch), "pv"], writes=[("cact", ch)])

            branches = [(w_a_out, uT, [("uT", c) for c in range(8)]),
                        (w_b_out, bmix, [("bmix", c) for c in range(8)]),
                        (w_c_out, cact, [("cact", c) for c in range(8)])]
            fence(RG("glxb", 8) + RG("gtail", 8) + ["dg"] + RG("conv", 8), RG("merged", 16) + RG("macc", 2) + R_XB)
            gs_ctr = 0
            for dp in range(8):
                for bi, (wbr, src, sregs) in enumerate(branches):
                    col = 5120 + bi * 2048 + dp * 256
                    s1, Wg = load_panel(w_in[l, col // 256], 16, 256)
                    s2, Wo = load_panel(wbr[l, dp], 8, 256)
                    for cc in range(2):
                        dc = dp * 2 + cc
                        for (c0, n) in blocks_m:
                            bg = mm_fm(Wg, s1, 16, cc, xT, TREG, c0, n)
                            gsb = gsig[gs_ctr % 3]
                            greg = ("gsig", gs_ctr % 3)
                            gs_ctr += 1
                            S.op("act", ACT(gsb[:, 0:n], ps[bg][:, 0:n], AF.Sigmoid,
                                            bias=PV(l, PV_BIN + 40 + bi * 16 + dc), scale=1.0),
                                 reads=[("ps", bg), "pv"], writes=[greg])
                            by = mm_fm(Wo, s2, 8, cc, src, sregs, c0, n)
                            mreg = ("macc", cc)
                            ma = macc[:, cc, c0:c0 + n]
                            if bi == 0:
                                S.op("dve", TT(ma, ps[by][:, 0:n], gsb[:, 0:n], ALU.mult),
                                     reads=[("ps", by), greg], writes=[mreg])
                            else:
                                S.op("dve", TT(gsb[:, 0:n], ps[by][:, 0:n], gsb[:, 0:n], ALU.mult),
                                     reads=[("ps", by), greg], writes=[greg])
                                if bi == 1:
                                    S.op("dve", TT(ma, ma, gsb[:, 0:n], ALU.add), reads=[greg, mreg], writes=[mreg])
                                else:
                                    S.op("dve", TT(merged[:, dc, c0:c0 + n], ma, gsb[:, 0:n], ALU.add),
                                         reads=[greg, mreg], writes=[("merged", dc)])

            S.dma("sp", "lnbc", lnbc[:, 0:4096], lnv[l, 0:2].rearrange("a n -> (a n)").partition_broadcast(128), writes=["lnbc"])
            MREG = [("merged", c) for c in range(16)]
            for p in range(8):
                s, Wv = load_panel(w_out[l, p], 16, 256)
                for (t, R) in tiles_m:
                    b = bank()
                    for k in range(16):
                        S.op("pe", MM(ps[b][0:R, 0:256], merged[:, k, t * 128:t * 128 + R], Wv[:, k, :], k == 0, k == 15),
                             reads=[("w", s)] + MREG, writes=[("ps", b)])
                    xs = x[0:R, t, p * 256:(p + 1) * 256]
                    S.op("dve", STT(xs, xs, ALPHA, ps[b][0:R, 0:256], ALU.mult, ALU.add),
                         reads=[("ps", b), ("x", t)], writes=[("x", t)])
                    bn_piece(t, R, p, xs, ("x", t))
            ln_multi([(lambda o, n, t=t, R=R: x[0:R, t, o:o + n], R, ("x", t), t) for (t, R) in tiles_m], D, 0, D, pre=8)
            build_xT(tiles_m)
            for (t, R) in tiles_m:
                S.op("act", lambda e, t=t, R=R: e.mul(out=x[0:R, t, :], in_=x[0:R, t, :], mul=ALPHA),
                     reads=[("x", t)], writes=[("x", t)])

            S.dma("sp", "lnbc", lnbc[:, 0:4096], lnv[l, 2:4].rearrange("a n -> (a n)").partition_broadcast(128), writes=["lnbc"])
            fence(["meanb", "rstdb", "ptA", "ptB"], R_HT)
            def ffn_down(fb):
                hTb = hT[fb % 2]
                hreg = [("hT", fb % 2, 0), ("hT", fb % 2, 1)]
                s3, Wd = load_panel(w_down[l, fb], 2, 2048)
                for (t, R) in tiles_m:
                    for db in range(4):
                        b = bank()
                        for kc in range(2):
                            S.op("pe", MM(ps[b][0:R, 0:512], hTb[:, kc, t * 128:t * 128 + R], Wd[:, kc, db * 512:(db + 1) * 512],
                                          kc == 0, kc == 1),
                                 reads=[("w", s3)] + hreg, writes=[("ps", b)])
                        xs = x[0:R, t, db * 512:(db + 1) * 512]
                        S.op("dve", TT(xs, xs, ps[b][0:R, 0:512], ALU.add), reads=[("ps", b), ("x", t)], writes=[("x", t)])

            for fb in range(22):
                hTb = hT[fb % 2]
                hreg = [("hT", fb % 2, 0), ("hT", fb % 2, 1)]
                s1, Wgt = load_panel(w_up[l, fb], 16, 256)
                s2, Wup = load_panel(w_up[l, 22 + fb], 16, 256)
                for cc in range(2):
                    for (c0, n) in blocks_m:
                        bg = mm_fm(Wgt, s1, 16, cc, xT, TREG, c0, n)
                        gsb = gsig[gs_ctr % 3]
                        greg = ("gsig", gs_ctr % 3)
                        gs_ctr += 1
                        S.op("act", ACT(gsb[:, 0:n], ps[bg][:, 0:n], AF.Silu), reads=[("ps", bg)], writes=[greg])
                        bu = mm_fm(Wup, s2, 16, cc, xT, TREG, c0, n)
                        S.op("dve", TT(hTb[:, cc, c0:c0 + n], ps[bu][:, 0:n], gsb[:, 0:n], ALU.mult),
                             reads=[("ps", bu), greg], writes=[hreg[cc]])
                if fb >= 1:
                    ffn_down(fb - 1)
            ffn_down(21)
            ln_multi([(lambda o, n, t=t, R=R: x[0:R, t, o:o + n], R, ("x", t), t) for (t, R) in tiles_m], D, 0, D)

        def WB3(pv_, l, wcol):
            base = pv_[:, l, wcol:wcol + 30]
            return bass.AP(base.tensor, base.offset, [list(base.ap[0]), [0, 16], [1, 30]])

        for gi in range(2):
            g = groups[gi]
            for (t, R) in g["tiles"]:
                r0 = g["row0"] + t * 128
                S.dma("sp", "xin", x[0:R, t, :], xin[r0:r0 + R, :], writes=[("x", t)])
            for l in range(DEPTH):
                load_layer_consts(l)
                build_xT(g["tiles"])
                layer(gi, l)
            if gi == 0:
                for t in range(1, 5):
                    S.dma("sp", "o_y", yp[(t - 1) * 128:t * 128, :], x[:, t, :], reads=[("x", t)])
            else:
                for t in range(4):
                    S.dma("sp", "o_y", yp[512 + t * 128:512 + (t + 1) * 128, :], x[:, t, :], reads=[("x", t)])
                S.dma("sp", "o_y", ys, x[0:16, 4, :], reads=[("x", 4)])

        cnt = S.emit(final_waits=["o_y", "o_vs", "o_ps", "o_pp", "o_cs", "o_cp"])
    return nc


_NC = None


def _prep_inputs(inp):
    f = lambda a: np.ascontiguousarray(np.asarray(a, dtype=np.float32))
    xp = f(inp["x_prompt"])
    xs = f(inp["x_sample"])
    sp = f(inp["state_pool"])
    sc = f(inp["state_conv"])
    b_in = f(inp["b_in"])
    L = DEPTH
    pvec = np.zeros((L, 128, PV_N), np.float32)
    fm = lambda v: v.reshape(L, -1, 128).transpose(0, 2, 1)
    pvec[:, :, PV_BIN:PV_BIN + 88] = fm(b_in)
    pvec[:, :, PV_BSC:PV_BSC + 8] = fm(f(inp["b_scale"]))
    pvec[:, :, PV_CBD:PV_CBD + 8] = fm(f(inp["c_b_dw"]))
    pvec[:, :, PV_CLG:PV_CLG + 8] = fm(f(inp["c_ln_g"]))
    pvec[:, :, PV_CLB:PV_CLB + 8] = fm(f(inp["c_ln_b"]))
    wdw = f(inp["c_w_dw"])
    pvec[:, :, PV_WDW:PV_WDW + 248] = wdw.reshape(L, 31, 8, 128).transpose(0, 3, 2, 1).reshape(L, 128, 248)
    rowv = np.stack([f(inp["a_ln_g"]), f(inp["a_ln_b"]), b_in[:, 1024:2048]], axis=1)
    lnv = np.stack([f(inp["ln1_g"]), f(inp["ln1_b"]), f(inp["ln2_g"]), f(inp["ln2_b"])], axis=1)
    a_ws = f(inp["a_ws"])
    awsT = np.ascontiguousarray(a_ws.transpose(0, 3, 1, 2)).reshape(L, 128, 1024)
    a_bs = f(inp["a_bs"])
    absd = a_bs.reshape(L, 1024)
    aw00 = np.ascontiguousarray(a_ws[:, :, 0, 0]).reshape(L, 8)
    abs0 = np.ascontiguousarray(a_bs[:, :, 0]).reshape(L, 8)
    ident = np.eye(128, dtype=np.float32)
    causT = np.triu(np.ones((128, 128), np.float32))
    def tile_kn(w, K, NP):
        return np.ascontiguousarray(f(w).reshape(L, K, 128, NP, 256).transpose(0, 3, 2, 1, 4)).reshape(L, NP, 128, K * 256)

    w_down_t = np.ascontiguousarray(f(inp["w_ffn_down"]).reshape(L, 22, 2, 128, D).transpose(0, 1, 3, 2, 4)).reshape(L, 22, 128, 2 * D)
    b_wg_t = np.ascontiguousarray(f(inp["b_w_group"]).reshape(L, 4, 2, 128, 256).transpose(0, 3, 1, 2, 4)).reshape(L, 128, 2048)
    shared = {
        "ident": ident, "causT": causT,
        "w_in": tile_kn(inp["w_in"], 16, 44), "w_a_out": tile_kn(inp["w_a_out"], 8, 8),
        "w_b_out": tile_kn(inp["w_b_out"], 8, 8), "w_c_out": tile_kn(inp["w_c_out"], 8, 8),
        "w_out": tile_kn(inp["w_out"], 16, 8), "w_ffn_up": tile_kn(inp["w_ffn_up"], 16, 44), "w_ffn_down": w_down_t,
        "b_w_group": b_wg_t, "pvec": pvec, "rowv": np.ascontiguousarray(rowv), "lnv": np.ascontiguousarray(lnv),
        "awsT": awsT, "a_bs": np.ascontiguousarray(absd), "a_w00": aw00, "a_bs0": abs0,
    }
    in_maps = []
    for c in range(NCORES):
        seq, half = c // 2, c % 2
        xin = np.zeros((NTOK, D), np.float32)
        if half == 1:
            xin[0:128] = xp[seq, 896:1024]
        xin[128:1152] = xp[seq, half * 1024:(half + 1) * 1024]
        xin[1152:1168] = xs[c * 16:(c + 1) * 16, 0, :]
        mask = np.full((128, 1), float(half), np.float32)
        invc = np.zeros((128, 64), np.float32)
        for gq, w in enumerate(POOL_W):
            for tt in range(16):
                cntv = float(w) if half == 1 else float(min(tt + 1, w))
                invc[:, gq * 16 + tt] = 1.0 / cntv
        m = dict(shared)
        m.update({"xin": xin, "mask": mask, "invc": invc,
                  "st_pool": np.ascontiguousarray(sp[:, c * 16:(c + 1) * 16]),
                  "st_conv": np.ascontiguousarray(sc[:, c * 16:(c + 1) * 16])})
        in_maps.append(m)
    return in_maps


def kernel(**inputs):
    global _NC
    if _NC is None:
        _NC = build_program()
    in_maps = _prep_inputs(inputs)
    res = run_bass_kernel_spmd(_NC, in_maps, core_ids=list(range(NCORES)))
    R = res.results
    y_prompt = np.zeros((4, 2048, D), np.float32)
    y_sample = np.zeros((128, 1, D), np.float32)
    pool_p = np.zeros((DEPTH, 4, 15, 1024), np.float32)
    conv_p = np.zeros((DEPTH, 4, 30, 1024), np.float32)
    pool_s = np.zeros((DEPTH, 128, 15, 1024), np.float32)
    conv_s = np.zeros((DEPTH, 128, 30, 1024), np.float32)
    v_s = np.zeros((DEPTH, 128, 1, 1024), np.float32)
    for c in range(NCORES):
        seq, half = c // 2, c % 2
        r = R[c]
        y_prompt[seq, half * 1024:(half + 1) * 1024] = r["yp"]
        y_sample[c * 16:(c + 1) * 16, 0] = r["ys"]
        if half == 1:
            pool_p[:, seq] = r["o_pool_p"]
            conv_p[:, seq] = r["o_conv_p"]
        pool_s[:, c * 16:(c + 1) * 16] = r["o_pool_s"]
        conv_s[:, c * 16:(c + 1) * 16] = r["o_conv_s"]
        v_s[:, c * 16:(c + 1) * 16, 0] = r["o_v_s"]
    return (y_prompt, y_sample, pool_p, conv_p, pool_s, conv_s, v_s)
```
